# Optimizing a Trainium2 kernel written in Bass

```python
import jax, jax.numpy as jnp
from jax import lax
import numpy as np

D_MODEL = 2048
BATCH = 4
SEQ = 4096
DEPTH = 1
DEC_BATCH = 1
DEC_SEQ = 16384
PAST_LEN = 128

POOL_WINDOWS = (2, 4, 8, 16)
POOL_GROUPS = len(POOL_WINDOWS)
POOL_WIDTH = D_MODEL // 2
POOL_GROUP_DIM = POOL_WIDTH // POOL_GROUPS
HEAD_DIM = 128
ATTN_GROUPS = ((128, 1), (512, 4), (2048, 16))
HEADS_PER_GROUP = 4
N_HEADS = HEADS_PER_GROUP * len(ATTN_GROUPS)
ATTN_WIDTH = N_HEADS * HEAD_DIM
ATTN_OUT_WIDTH = HEADS_PER_GROUP * HEAD_DIM
ROPE_THETA = 10000.0
N_BRANCHES = 2
IN_WIDTH = POOL_WIDTH + 3 * ATTN_WIDTH + N_BRANCHES * D_MODEL
D_FF = ((8 * D_MODEL // 3 + 255) // 256) * 256
CONV_WIDTH = 3
EPS = 1e-6
NEG_INF = -1e30

kernel_name = 'hybrid_pool_dilated_attn_encoder'


def rms_norm(x, g):
    xf = x.astype(jnp.float32)
    inv = lax.rsqrt(jnp.mean(xf * xf, axis=-1, keepdims=True) + EPS)
    return (xf * inv * g.astype(jnp.float32)).astype(x.dtype)


def rope(x):
    S = x.shape[1]
    half = HEAD_DIM // 2
    inv_freq = ROPE_THETA ** (-jnp.arange(half, dtype=jnp.float32) / half)
    ang = jnp.arange(S, dtype=jnp.float32)[:, None] * inv_freq[None, :]
    bshape = (1, S) + (1,) * (x.ndim - 3) + (half,)
    cos = jnp.cos(ang).reshape(bshape)
    sin = jnp.sin(ang).reshape(bshape)
    xf = x.astype(jnp.float32)
    x1, x2 = xf[..., :half], xf[..., half:]
    return jnp.concatenate([x1 * cos - x2 * sin, x2 * cos + x1 * sin], axis=-1).astype(x.dtype)


def pool_mixer(u, w_pool, pool_scale):
    B, S, _ = u.shape
    ug = u.reshape(B, S, POOL_GROUPS, POOL_GROUP_DIM).astype(jnp.float32)
    csum = jnp.concatenate([jnp.zeros((B, 1, POOL_GROUPS, POOL_GROUP_DIM), jnp.float32),
                            jnp.cumsum(ug, axis=1)], axis=1)
    pos = jnp.arange(S)[:, None]
    radius = jnp.array(POOL_WINDOWS, dtype=jnp.int32)[None, :] // 2
    lo = jnp.clip(pos - radius, 0, S)
    hi = jnp.clip(pos + radius + 1, 0, S)
    grp = jnp.arange(POOL_GROUPS)[None, :]
    count = (hi - lo).astype(jnp.float32)
    mean = (csum[:, hi, grp, :] - csum[:, lo, grp, :]) / count[None, :, :, None]
    mixed = jnp.einsum('bsgc,gce->bsge', mean - ug, w_pool.astype(jnp.float32))
    return (mixed.reshape(B, S, POOL_WIDTH) * pool_scale.astype(jnp.float32)).astype(u.dtype)


def dilated_window_attention(q, k, v, window, dilation):
    B, S, H, Dh = q.shape
    radius = window // (2 * dilation)
    blk = radius
    L = S // dilation
    nb = -(-L // blk)
    Lp = nb * blk
    N = B * dilation

    def to_sub(t):
        t = t.reshape(B, L, dilation, H, Dh).transpose(0, 2, 1, 3, 4).reshape(N, L, H, Dh)
        return jnp.pad(t, ((0, 0), (0, Lp - L), (0, 0), (0, 0))).reshape(N, nb, blk, H, Dh)

    def with_neighbours(tb):
        tp = jnp.pad(tb, ((0, 0), (1, 1), (0, 0), (0, 0), (0, 0)))
        return jnp.concatenate([tp[:, :-2], tp[:, 1:-1], tp[:, 2:]], axis=2)

    qb = to_sub(q)
    kb = with_neighbours(to_sub(k))
    vb = with_neighbours(to_sub(v))
    qpos = jnp.arange(Lp).reshape(nb, blk)
    kpos = (jnp.arange(nb)[:, None] - 1) * blk + jnp.arange(3 * blk)[None, :]
    rel = kpos[:, None, :] - qpos[:, :, None]
    valid = (jnp.abs(rel) <= radius) & (kpos[:, None, :] >= 0) & (kpos[:, None, :] < L)
    s = jnp.einsum('nbqhd,nbkhd->nbhqk', qb, kb,
                   preferred_element_type=jnp.float32) * (HEAD_DIM ** -0.5)
    s = jnp.where(valid[None, :, None], s, NEG_INF)
    m = jnp.max(s, axis=-1, keepdims=True)
    p = jnp.exp(s - m)
    denom = jnp.sum(p, axis=-1)
    o = jnp.einsum('nbhqk,nbkhd->nbqhd', p, vb.astype(jnp.float32))
    o = o / jnp.transpose(denom, (0, 1, 3, 2))[..., None]
    lse = jnp.transpose(m[..., 0] + jnp.log(denom), (0, 1, 3, 2))

    def from_sub(t):
        t = t.reshape((N, Lp) + t.shape[3:])[:, :L]
        t = t.reshape((B, dilation, L) + t.shape[2:])
        t = jnp.swapaxes(t, 1, 2)
        return t.reshape((B, S) + t.shape[3:])

    return from_sub(o), from_sub(lse)


def attention_mixer(q, k, v, q_norm_g, k_norm_g):
    B, S, _ = q.shape
    shape = (B, S, len(ATTN_GROUPS), HEADS_PER_GROUP, HEAD_DIM)
    qh = rope(rms_norm(q.reshape(shape), q_norm_g))
    kh = rope(rms_norm(k.reshape(shape), k_norm_g))
    vh = v.reshape(shape)
    outs, lses = [], []
    for g, (window, dilation) in enumerate(ATTN_GROUPS):
        o, l = dilated_window_attention(qh[:, :, g], kh[:, :, g], vh[:, :, g], window, dilation)
        outs.append(o)
        lses.append(l)
    outs = jnp.stack(outs)
    wts = jax.nn.softmax(jnp.stack(lses), axis=0)
    out = jnp.sum(wts[..., None] * outs, axis=0)
    return out.reshape(B, S, ATTN_OUT_WIDTH).astype(q.dtype)


def conv_ffn(h, w_up, conv_w, conv_b, w_down):
    S = h.shape[1]
    gate, val = jnp.split(h @ w_up, 2, axis=-1)
    half = CONV_WIDTH // 2
    gp = jnp.pad(gate, ((0, 0), (half, half), (0, 0)))
    conv = conv_b
    for t in range(CONV_WIDTH):
        conv = conv + gp[:, t:t + S] * conv_w[t]
    return (jax.nn.gelu(conv) * val) @ w_down


def encoder_layer(x, mix_norm_g, w_in, b_gate, q_norm_g, k_norm_g, w_pool, pool_scale,
                  w_pool_out, w_attn_out, w_o, ffn_norm_g, w_up, conv_w, conv_b, w_down):
    B, S, _ = x.shape
    h = rms_norm(x, mix_norm_g)
    z = h @ w_in
    cuts = [POOL_WIDTH, POOL_WIDTH + ATTN_WIDTH, POOL_WIDTH + 2 * ATTN_WIDTH, POOL_WIDTH + 3 * ATTN_WIDTH]
    u_pool, q, k, v, g = jnp.split(z, cuts, axis=-1)
    pool_d = pool_mixer(u_pool, w_pool, pool_scale) @ w_pool_out
    attn_d = attention_mixer(q, k, v, q_norm_g, k_norm_g) @ w_attn_out
    gates = jax.nn.sigmoid((g + b_gate).astype(jnp.float32)).astype(x.dtype)
    gates = gates.reshape(B, S, N_BRANCHES, D_MODEL)
    merged = gates[:, :, 0] * pool_d + gates[:, :, 1] * attn_d
    x = x + merged @ w_o
    x = x + conv_ffn(rms_norm(x, ffn_norm_g), w_up, conv_w, conv_b, w_down)
    return x


def setup_inputs(seed: int = 0) -> dict:
    key = jax.random.key(seed)
    ks = jax.random.split(key, 18)
    f32 = jnp.float32
    nrm = lambda k, shape: jax.random.normal(k, shape, f32)
    return {
        'x_prompt': nrm(ks[0], (BATCH, SEQ, D_MODEL)),
        'x_sample': nrm(ks[1], (DEC_BATCH, DEC_SEQ, D_MODEL)),
        'mix_norm_g': 1.0 + 0.02 * nrm(ks[2], (DEPTH, D_MODEL)),
        'w_in': nrm(ks[3], (DEPTH, D_MODEL, IN_WIDTH)) * D_MODEL ** -0.5,
        'b_gate': 0.01 * nrm(ks[4], (DEPTH, N_BRANCHES * D_MODEL)),
        'q_norm_g': 1.0 + 0.02 * nrm(ks[5], (DEPTH, HEAD_DIM)),
        'k_norm_g': 1.0 + 0.02 * nrm(ks[6], (DEPTH, HEAD_DIM)),
        'w_pool': nrm(ks[7], (DEPTH, POOL_GROUPS, POOL_GROUP_DIM, POOL_GROUP_DIM)) * POOL_GROUP_DIM ** -0.5,
        'pool_scale': 1.0 + 0.1 * nrm(ks[8], (DEPTH, POOL_WIDTH)),
        'w_pool_out': nrm(ks[9], (DEPTH, POOL_WIDTH, D_MODEL)) * POOL_WIDTH ** -0.5,
        'w_attn_out': nrm(ks[10], (DEPTH, ATTN_OUT_WIDTH, D_MODEL)) * ATTN_OUT_WIDTH ** -0.5,
        'w_o': nrm(ks[11], (DEPTH, D_MODEL, D_MODEL)) * D_MODEL ** -0.5,
        'ffn_norm_g': 1.0 + 0.02 * nrm(ks[12], (DEPTH, D_MODEL)),
        'w_up': nrm(ks[13], (DEPTH, D_MODEL, 2 * D_FF)) * D_MODEL ** -0.5,
        'conv_w': nrm(ks[14], (DEPTH, CONV_WIDTH, D_FF)) * CONV_WIDTH ** -0.5,
        'conv_b': 0.01 * nrm(ks[15], (DEPTH, D_FF)),
        'w_down': nrm(ks[16], (DEPTH, D_FF, D_MODEL)) * D_FF ** -0.5,
    }


def reference(x_prompt, x_sample, mix_norm_g, w_in, b_gate, q_norm_g, k_norm_g, w_pool, pool_scale,
              w_pool_out, w_attn_out, w_o, ffn_norm_g, w_up, conv_w, conv_b, w_down):
    def trunk(x):
        for l in range(DEPTH):
            x = encoder_layer(x, mix_norm_g[l], w_in[l], b_gate[l], q_norm_g[l], k_norm_g[l],
                              w_pool[l], pool_scale[l], w_pool_out[l], w_attn_out[l], w_o[l],
                              ffn_norm_g[l], w_up[l], conv_w[l], conv_b[l], w_down[l])
        return x
    y_prompt = trunk(x_prompt)
    y_sample = trunk(x_sample)
    return (y_prompt, y_sample)
```

```python
import math
from contextlib import ExitStack

import numpy as np
import ml_dtypes

import concourse.bass as bass
import concourse.mybir as mybir
from concourse.bass_utils import run_bass_kernel_spmd

F32 = mybir.dt.float32
BF16 = mybir.dt.bfloat16
AF = mybir.ActivationFunctionType
ALU = mybir.AluOpType
AX = mybir.AxisListType

NCORES = 8
D = 2048
KC = 16
HD = 128
POOLW = 1024
AW = 1536
INW = 9728
DFF = 5632
FC = 44
QOFF, KOFF, VOFF, GOFF = 1024, 2560, 4096, 5632
HALO = 1025
EPS = 1e-6
DIL = (1, 4, 16)
POOL_R = (1, 2, 4, 8)
ROPE_THETA = 10000.0


class Cfg:
    def __init__(self, ntc=4096, debug=False, sweeps=(0, 1, 2, 3)):
        assert ntc % 512 == 0
        self.NTC = ntc
        self.NMIX = ntc + 2
        self.NE = ntc + 2050
        self.debug = debug
        self.sweeps = sweeps
        self.s1_tiles = [(u0, min(512, self.NE - u0)) for u0 in range(0, self.NE, 512)]
        MT = 464
        sizes = []
        rem = self.NMIX
        while rem > 642:
            sizes.append(512)
            rem -= 512
        if rem > 512:
            sizes.append(384)
            rem -= 384
        sizes.append(rem)
        starts = [1024 + sum(sizes[:i]) for i in range(len(sizes))]
        self.mix_tiles = list(zip(starts, sizes))
        self.ffn_tiles = [512 * j for j in range(ntc // 512)]


def subtiles(n):
    return [(o, min(128, n - o)) for o in range(0, n, 128)]


def attn_units(ntok):
    out = []
    for g, Dg in enumerate(DIL):
        for r in range(min(Dg, ntok)):
            nl = (ntok - r + Dg - 1) // Dg
            for l0 in range(0, nl, 128):
                out.append((g, r, l0, min(128, nl - l0)))
    return out


class Buf:
    __slots__ = ("name", "w", "r")

    def __init__(self, name=""):
        self.name = name
        self.w = {}
        self.r = {}


class Eng:
    def __init__(self, be, name, sem, key):
        self.be = be
        self.name = name
        self.sem = sem
        self.key = key
        self.cnt = 0
        self.pend = False
        self.waited = {}
        self.pool = []
        self.rr = 0


def _merge(d, s):
    for k, v in s.items():
        if d.get(k, 0) < v:
            d[k] = v


class Sched:
    def __init__(self, nc, es, ndma=(12, 12, 8)):
        self.nc = nc
        self.semh = {}
        self.dcnt = {}
        nk = [0]

        def newsem(name):
            k = nk[0]
            nk[0] += 1
            self.semh[k] = es.enter_context(nc.semaphore(name))
            return k

        self.pe = Eng(nc.tensor, "pe", None, newsem("s_pe"))
        self.act = Eng(nc.scalar, "act", None, newsem("s_act"))
        self.dve = Eng(nc.vector, "dve", None, newsem("s_dve"))
        self.pool = Eng(nc.gpsimd, "pool", None, newsem("s_pool"))
        self.sp = Eng(nc.sync, "sp", None, newsem("s_sp"))
        self.engs = [self.pe, self.act, self.dve, self.pool, self.sp]
        for e in self.engs:
            e.sem = self.semh[e.key]
        for e, n in ((self.sp, ndma[0]), (self.pool, ndma[1]), (self.act, ndma[2])):
            for i in range(n):
                k = newsem(f"d_{e.name}{i}")
                e.pool.append(k)
                self.dcnt[k] = 0

    def _wait(self, eng, deps):
        for k, v in deps.items():
            if k == eng.key and eng is self.pe:
                continue
            if eng.waited.get(k, 0) < v:
                eng.be.wait_ge(self.semh[k], v)
                eng.waited[k] = v

    @staticmethod
    def _deps(reads, writes, adds):
        deps = {}
        for b in reads:
            _merge(deps, b.w)
        for b in writes:
            _merge(deps, b.w)
            _merge(deps, b.r)
        for b in adds:
            _merge(deps, b.r)
        return deps

    @staticmethod
    def _commit(tokk, tokv, reads, writes, adds):
        for b in reads:
            if b.r.get(tokk, 0) < tokv:
                b.r[tokk] = tokv
        for b in writes:
            b.w = {tokk: tokv}
            b.r = {}
        for b in adds:
            if b.w.get(tokk, 0) < tokv:
                b.w[tokk] = tokv

    def op(self, eng, fn, reads=(), writes=(), adds=(), inc=True):
        self._wait(eng, self._deps(reads, writes, adds))
        ins = fn(eng.be)
        if inc:
            eng.cnt += 1
            ins.then_inc(eng.sem, 1)
            tokv = eng.cnt
        else:
            tokv = eng.cnt + 1
        self._commit(eng.key, tokv, reads, writes, adds)
        return ins

    def dma(self, q, out, in_, reads=(), writes=(), adds=(), nc_ok=False, pool=None):
        self._wait(q, self._deps(reads, writes, adds))
        if pool is None:
            k = q.pool[q.rr]
            q.rr = (q.rr + 1) % len(q.pool)
        else:
            k = pool[0][pool[1] % len(pool[0])]
            pool[1] += 1
        prev = self.dcnt[k]
        if prev and q.waited.get(k, 0) < prev:
            q.be.wait_ge(self.semh[k], prev)
            q.waited[k] = prev
        self.dcnt[k] = prev + 16
        if nc_ok:
            with self.nc.allow_non_contiguous_dma(reason="small strided table"):
                q.be.dma_start(out=out, in_=in_).then_inc(self.semh[k], 16)
        else:
            q.be.dma_start(out=out, in_=in_).then_inc(self.semh[k], 16)
        self._commit(k, prev + 16, reads, writes, adds)

    def barrier(self, skip=(), skip_sems=()):
        deps = {}
        skipk = set(skip_sems)
        for e in skip:
            skipk.update(e.pool)
        for e in self.engs:
            if e.cnt and e not in skip:
                deps[e.key] = e.cnt
        for k, v in self.dcnt.items():
            if v and k not in skipk:
                deps[k] = v
        for e in self.engs:
            if e not in skip:
                self._wait(e, deps)


class T:
    def __init__(self, t, name):
        self.t = t
        self.b = Buf(name)

    def __getitem__(self, k):
        return self.t[k]


def build(cfg):
    nc = bass.Bass("TRN2", target_bir_lowering=False)
    NE, NTC, NMIX = cfg.NE, cfg.NTC, cfg.NMIX
    dbg = cfg.debug

    def din(name, shape, dt=F32):
        return nc.dram_tensor(name, list(shape), dt, kind="ExternalInput").ap()

    def dscr(name, shape, dt, out=False):
        kind = "ExternalOutput" if (out and dbg) else "Internal"
        return nc.dram_tensor(name, list(shape), dt, kind=kind).ap()

    xe = din("xe", [NE, D])
    w_in = din("w_in", [D, INW])
    w_pool = din("w_pool", [POOLW, 256])
    w_po = din("w_po", [POOLW, D])
    w_ao = din("w_ao", [512, D])
    w_o = din("w_o", [D, D])
    w_up = din("w_up", [D, 2 * DFF])
    w_down = din("w_down", [DFF, D])
    g1 = din("g1", [1, D])
    g2 = din("g2", [1, D])
    gq = din("gq", [1, HD])
    gk = din("gk", [1, HD])
    b_gate = din("b_gate", [2 * D])
    pscale = din("pscale", [POOLW])
    conv_w = din("conv_w", [3, DFF])
    conv_b = din("conv_b", [DFF])
    ident_d = din("ident", [128, 128], BF16)
    ones_d = din("ones", [128, 128], BF16)
    mask_d = din("mask12", [128, 256])
    negm_d = din("negm", [128, 1024], BF16)
    ropeC = din("ropeC", [NE, HD])
    ropeS = din("ropeS", [NE, HD])
    tokv = din("tokvalid", [1, NE])
    invc = din("invcnt", [4, NE])
    nkt = max(len(attn_units(n)) for _, n in cfg.mix_tiles) * 2
    kvtab = din("kvtab", [len(cfg.mix_tiles), 128, nkt])

    y = nc.dram_tensor("y", [NTC, D], F32, kind="ExternalOutput").ap()

    wb_in = dscr("wb_in", [D, INW], BF16)
    wb_pool = dscr("wb_pool", [POOLW, 256], BF16)
    wb_po = dscr("wb_po", [POOLW, D], BF16)
    wb_ao = dscr("wb_ao", [512, D], BF16)
    wb_o = dscr("wb_o", [D, D], BF16)
    wb_up = dscr("wb_up", [D, 2 * DFF], BF16)
    wb_down = dscr("wb_down", [DFF, D], BF16)
    KT = dscr("KT", [12, HD, NE], BF16, out=True)
    V = dscr("V", [NE, AW], BF16, out=True)
    X1 = dscr("X1", [NMIX, D], F32, out=True)

    es = ExitStack()
    with es:
        S = Sched(nc, es)
        PE, ACT, DVE, POOL, SP = S.pe, S.act, S.dve, S.pool, S.sp
        castpool = [POOL.pool[-3:], 0]
        POOL.pool = POOL.pool[:-3]

        def sb(stack, name, shape, dt):
            return T(stack.enter_context(nc.sbuf_tensor("sb_" + name, list(shape), dt)), name)

        ident = sb(es, "ident", [128, 128], BF16)
        ones = sb(es, "ones", [128, 128], BF16)
        mask12 = sb(es, "mask12", [128, 256], F32)
        negm = sb(es, "negm", [128, 2, 4, 128], BF16)
        gq4 = sb(es, "gq4", [128, 512], F32)
        gk4 = sb(es, "gk4", [128, 512], F32)
        bgh = sb(es, "bgh", [128, 32], F32)
        psh = sb(es, "psh", [128, 8], F32)
        cw = sb(es, "cw", [128, 3, FC], F32)
        cbv = sb(es, "cbv", [128, FC], F32)
        cst = sb(es, "cst", [128, 16], F32)
        es_g1 = ExitStack()
        gb1 = sb(es_g1, "gb1", [128, D], F32)
        NW = 3
        wring = []
        wstate = {"i": 0}
        psum = [T(es.enter_context(nc.psum_tensor(f"ps{i}", [128, 512], F32)), f"ps{i}") for i in range(8)]

        def psbf(i):
            return psum[i].t[:, :].bitcast(BF16).rearrange("p (c t) -> p c t", c=8)

        S.dma(POOL, ident[:, :], ident_d[:, :], writes=[ident.b])
        S.dma(POOL, ones[:, :], ones_d[:, :], writes=[ones.b])
        S.dma(POOL, mask12[:, :], mask_d[:, :], writes=[mask12.b])
        S.dma(POOL, negm[:, :, :, :], negm_d.rearrange("p (t h i) -> p t h i", t=2, h=4), writes=[negm.b])
        S.dma(POOL, gb1[:, :], g1[0:1, :].partition_broadcast(128), writes=[gb1.b])
        for h in range(4):
            S.dma(POOL, gq4[:, h * 128:(h + 1) * 128], gq[0:1, :].partition_broadcast(128), adds=[gq4.b])
            S.dma(POOL, gk4[:, h * 128:(h + 1) * 128], gk[0:1, :].partition_broadcast(128), adds=[gk4.b])
        S.dma(POOL, bgh[:, :], b_gate.rearrange("(c p) -> p c", p=128), writes=[bgh.b], nc_ok=True)
        S.dma(POOL, psh[:, :], pscale.rearrange("(c p) -> p c", p=128), writes=[psh.b], nc_ok=True)
        for t3 in range(3):
            S.dma(POOL, cw[:, t3, :], conv_w[t3, :].rearrange("(c p) -> p c", p=128), adds=[cw.b], nc_ok=True)
        S.dma(POOL, cbv[:, :], conv_b.rearrange("(c p) -> p c", p=128), writes=[cbv.b], nc_ok=True)
        S.op(DVE, lambda e: e.memset(cst[:, 0:8], EPS), writes=[cst.b])
        S.op(ACT, lambda e: e.mul(bgh[:, :], bgh[:, :], 0.5), writes=[bgh.b])
        S.op(ACT, lambda e: e.mul(psh[:, :], psh[:, :], 0.5), writes=[psh.b])

        wbuf = {n: Buf(n) for n in ("in_kv", "in_rest", "pool", "po", "ao", "o", "up", "down")}
        casts = []

        def cast_rows(dst, src, bufname, c0=None, c1=None, rows=128, cols=2048):
            nrows = src.shape[0]
            if c0 is None:
                c0, c1 = 0, src.shape[1]
            for r0 in range(0, nrows, rows):
                r1 = min(nrows, r0 + rows)
                for cc in range(c0, c1, cols):
                    ce = min(c1, cc + cols)
                    casts.append((dst[r0:r1, cc:ce], src[r0:r1, cc:ce], bufname))

        n_first = 0
        cast_rows(wb_in, w_in, "in_rest", 0, KOFF)
        cast_rows(wb_in, w_in, "in_rest", GOFF, INW)
        cast_rows(wb_pool, w_pool, "pool")
        cast_rows(wb_po, w_po, "po")
        cast_rows(wb_ao, w_ao, "ao")
        cast_rows(wb_o, w_o, "o")
        cast_rows(wb_up, w_up, "up")
        cast_rows(wb_down, w_down, "down")
        cast_pos = {"i": 0}

        def emit_casts(n):
            while n > 0 and cast_pos["i"] < len(casts):
                dst, src, bn = casts[cast_pos["i"]]
                cast_pos["i"] += 1
                S.dma(POOL, dst, src, adds=[wbuf[bn]], pool=castpool)
                n -= 1

        wkv_es = ExitStack()
        if 1 in cfg.sweeps:
            wkv = sb(wkv_es, "wkv", [128, KC, 6 * 512], BF16)
            wkvb = [Buf(f"wkv{cb}") for cb in range(6)]
            for cb in (0, 3, 1, 4, 2, 5):
                S.dma(POOL, wkv[:, :, cb * 512:(cb + 1) * 512],
                      w_in[:, KOFF + cb * 512:KOFF + (cb + 1) * 512].rearrange("(k p) c -> p k c", p=128),
                      writes=[wkvb[cb]])
        if 0 in cfg.sweeps:
            emit_casts(len(casts))

        def wload(wb, wname, k0, nk, c0, ncols=512):
            slot = wring[wstate["i"] % NW]
            wstate["i"] += 1
            src = wb[k0 * 128:(k0 + nk) * 128, c0:c0 + ncols].rearrange("(k p) c -> p k c", p=128)
            S.dma(SP, slot[:, 0:nk, 0:ncols], src, reads=[wbuf[wname]], writes=[slot.b])
            return slot

        def wload_into(slot, kofs, wb, wname, k0, nk, c0, ncols=512):
            src = wb[k0 * 128:(k0 + nk) * 128, c0:c0 + ncols].rearrange("(k p) c -> p k c", p=128)
            S.dma(SP, slot[:, kofs:kofs + nk, 0:ncols], src, reads=[wbuf[wname]], adds=[slot.b])

        LQ = {"q": SP}
        kvb = Buf("kvscratch")
        x1b = [Buf(f"x1_{i}") for i in range(len(cfg.mix_tiles))]

        def x1_deps(m_lo, n):
            return [x1b[i] for i, (a_, nt_) in enumerate(cfg.mix_tiles)
                    if a_ - 1024 < m_lo + n and a_ - 1024 + nt_ > m_lo]
        uid = {"n": 0}

        def un(name):
            uid["n"] += 1
            return f"{name}_{uid['n']}"

        def cbarrier():
            S.barrier(skip=(SP,))

        def norm_stages(stk_bufs, src_fn, n, gbt, hT, hTb, col0, src_reads=()):
            xt, hb, ssb = stk_bufs
            out = []
            for si, (off, nt) in enumerate(subtiles(n)):
                s = si % 2

                def n1(si=si, off=off, nt=nt, s=s):
                    S.dma(LQ['q'], xt[s][:nt, :], src_fn(off, nt), reads=list(src_reads), writes=[xt[s].b])
                    S.op(ACT, lambda e: e.activation(out=hb[s][:nt, :], in_=xt[s][:nt, :], func=AF.Square,
                                                     accum_out=ssb[s][:nt, 0:1]),
                         reads=[xt[s].b], writes=[hb[s].b, ssb[s].b])
                    S.op(ACT, lambda e: e.activation(out=ssb[s][:nt, 1:2], in_=ssb[s][:nt, 0:1], func=AF.Sqrt,
                                                     bias=cst[:nt, 0:1], scale=1.0 / D),
                         reads=[cst.b], writes=[ssb[s].b])
                    S.op(DVE, lambda e: e.reciprocal(out=ssb[s][:nt, 1:2], in_=ssb[s][:nt, 1:2]), writes=[ssb[s].b])
                    S.op(DVE, lambda e: e.scalar_tensor_tensor(out=hb[s][:nt, :], in0=xt[s][:nt, :],
                                                               scalar=ssb[s][:nt, 1:2], in1=gbt[:nt, :],
                                                               op0=ALU.mult, op1=ALU.mult),
                         reads=[xt[s].b, ssb[s].b, gbt.b], writes=[hb[s].b])

                def n2(si=si, off=off, nt=nt, s=s):
                    for half in range(2):
                        pb = psbf(half)
                        for c8 in range(8):
                            c = half * 8 + c8
                            S.op(PE, lambda e: e.transpose(pb[:, c8, :nt], hb[s][:nt, c * 128:(c + 1) * 128],
                                                           ident[:nt, :nt]),
                                 reads=[hb[s].b, ident.b], writes=[psum[half].b], inc=(c8 == 7))
                        dst = hT[:, half * 8:(half + 1) * 8, col0 + off:col0 + off + nt]
                        if half == 0:
                            S.op(ACT, lambda e: e.activation(out=dst, in_=pb[:, :, :nt], func=AF.Copy),
                                 reads=[psum[half].b], adds=[hTb[si]])
                        else:
                            S.op(DVE, lambda e: e.tensor_copy(out=dst, in_=pb[:, :, :nt]),
                                 reads=[psum[half].b], adds=[hTb[si]])
                out.append((n1, n2))
            return out

        def norm_T(stk_bufs, src_fn, n, gbt, hT, hTb, col0, src_reads=()):
            for n1, n2 in norm_stages(stk_bufs, src_fn, n, gbt, hT, hTb, col0, src_reads):
                n1()
                n2()

        def qk_norm_rope(bufs, src_ps, nt, g4, rc, rs_, outb, beng=None):
            junk, ss4, kn, A, B = bufs
            beng = beng or DVE
            for h in range(4):
                S.op(ACT, lambda e: e.activation(out=junk[:nt, h * 128:(h + 1) * 128],
                                                 in_=src_ps.t[:nt, h * 128:(h + 1) * 128], func=AF.Square,
                                                 accum_out=ss4[:nt, h:h + 1]),
                     reads=[src_ps.b], adds=[junk.b, ss4.b] if h else [junk.b], writes=[] if h else [ss4.b])
            S.op(ACT, lambda e: e.activation(out=ss4[:nt, 4:8], in_=ss4[:nt, 0:4], func=AF.Sqrt,
                                             bias=cst[:nt, 0:1], scale=1.0 / HD), reads=[cst.b], writes=[ss4.b])
            S.op(DVE, lambda e: e.reciprocal(out=ss4[:nt, 4:8], in_=ss4[:nt, 4:8]), writes=[ss4.b])
            for h in range(4):
                S.op(DVE, lambda e: e.scalar_tensor_tensor(out=kn[:nt, h * 128:(h + 1) * 128],
                                                           in0=src_ps.t[:nt, h * 128:(h + 1) * 128],
                                                           scalar=ss4[:nt, 4 + h:5 + h], in1=g4[:nt, h * 128:(h + 1) * 128],
                                                           op0=ALU.mult, op1=ALU.mult),
                     reads=[src_ps.b, ss4.b, g4.b], adds=[kn.b] if h else [], writes=[] if h else [kn.b])
            S.op(DVE, lambda e: e.tensor_tensor(out=A[:nt, :].rearrange("p (h d) -> p h d", h=4),
                                                in0=kn[:nt, :].rearrange("p (h d) -> p h d", h=4),
                                                in1=rc[:nt, :].unsqueeze(1).to_broadcast([nt, 4, HD]), op=ALU.mult),
                 reads=[kn.b, rc.b], writes=[A.b])
            knv = kn[:nt, :].rearrange("p (h t d) -> p h t d", h=4, t=2)
            Bv = B[:nt, :].rearrange("p (h t d) -> p h t d", h=4, t=2)
            S.op(beng, lambda e: e.tensor_tensor(out=Bv[:, :, 0, :], in0=knv[:, :, 1, :],
                                                 in1=rs_[:nt, 0:64].unsqueeze(1).to_broadcast([nt, 4, 64]),
                                                 op=ALU.mult), reads=[kn.b, rs_.b], writes=[B.b])
            S.op(beng, lambda e: e.tensor_tensor(out=Bv[:, :, 1, :], in0=knv[:, :, 0, :],
                                                 in1=rs_[:nt, 64:128].unsqueeze(1).to_broadcast([nt, 4, 64]),
                                                 op=ALU.mult), reads=[kn.b, rs_.b], adds=[B.b])
            S.op(DVE, lambda e: e.tensor_tensor(out=outb[:nt, :], in0=A[:nt, :], in1=B[:nt, :], op=ALU.add),
                 reads=[A.b, B.b], writes=[outb.b])

        def load_rope(rc, rs_, u0, nt):
            S.dma(LQ['q'], rc[:nt, :], ropeC[u0:u0 + nt, :], writes=[rc.b])
            S.dma(LQ['q'], rs_[:nt, :], ropeS[u0:u0 + nt, :], writes=[rs_.b])

        if 1 in cfg.sweeps:
            with ExitStack() as s1:
                xt = [sb(s1, f"xt{i}", [128, D], F32) for i in range(2)]
                hb = [sb(s1, f"hb{i}", [128, D], BF16) for i in range(2)]
                ssb = [sb(s1, f"ssb{i}", [128, 2], F32) for i in range(2)]
                hT2 = [sb(s1, f"hT{i}", [128, KC, 512], BF16) for i in range(2)]
                hTb2 = [[Buf(f"hT{k}_{i}") for i in range(4)] for k in range(2)]
                qkbufs = [(sb(s1, f"junk{i}", [128, 512], BF16), sb(s1, f"ss4{i}", [128, 8], F32),
                           sb(s1, f"kn{i}", [128, 512], F32), sb(s1, f"A{i}", [128, 512], F32),
                           sb(s1, f"B{i}", [128, 512], F32)) for i in range(2)]
                kcnt = 0
                rc = [sb(s1, f"rc{i}", [128, HD], F32) for i in range(8)]
                rs_ = [sb(s1, f"rs{i}", [128, HD], F32) for i in range(8)]
                kb = [sb(s1, f"kb{i}", [128, 512], BF16) for i in range(3)]
                vb = [sb(s1, f"vb{i}", [128, 512], BF16) for i in range(2)]
                kst = [sb(s1, f"kst{i}", [128, 4, 512], BF16) for i in range(2)]
                cnt = 0
                def s1_norm(ti):
                    u0_, ntok_ = cfg.s1_tiles[ti]
                    return norm_stages((xt, hb, ssb), lambda off, nt, u0_=u0_: xe[u0_ + off:u0_ + off + nt, :], ntok_,
                                       gb1, hT2[ti % 2], hTb2[ti % 2], 0)

                for n1, n2 in s1_norm(0):
                    n1()
                    n2()
                for ti, (u0, ntok) in enumerate(cfg.s1_tiles):
                    sub = subtiles(ntok)
                    hT, hTb = hT2[ti % 2], hTb2[ti % 2]
                    nsched = {}
                    if ti + 1 < len(cfg.s1_tiles):
                        for k, (n1, n2) in enumerate(s1_norm(ti + 1)):
                            nsched.setdefault(4 + 3 * k, []).append(n1)
                            nsched.setdefault(6 + 3 * k, []).append(n2)
                    rcs, rss = rc[(ti % 2) * 4:(ti % 2) * 4 + 4], rs_[(ti % 2) * 4:(ti % 2) * 4 + 4]
                    for si, (off, nt) in enumerate(sub):
                        load_rope(rcs[si], rss[si], u0 + off, nt)
                    pend = []
                    ucount = 0
                    for cb in (0, 3, 1, 4, 2, 5):
                        lo_, hi_ = 1024 - 64 * DIL[cb % 3], 1024 + NTC + 2 + 64 * DIL[cb % 3]
                        act = [(si, off, nt) for si, (off, nt) in enumerate(sub)
                               if u0 + off < hi_ and u0 + off + nt > lo_]
                        for ai, (si, off, nt) in enumerate(act):
                            first_, last_ = (ai == 0), (ai == len(act) - 1)
                            c_lo, c_hi = act[0][1], act[-1][1] + act[-1][2]
                            bank = psum[2 + (cnt % 4)]
                            cnt += 1
                            for kc in range(KC):
                                S.op(PE, lambda e: e.matmul(bank.t[:nt, :], lhsT=hT[:, kc, off:off + nt],
                                                            rhs=wkv[:, kc, cb * 512:(cb + 1) * 512], start=(kc == 0),
                                                            stop=(kc == KC - 1)),
                                     reads=[hTb[si], wkvb[cb]], writes=[bank.b], inc=(kc == KC - 1))
                            if cb < 3:
                                kbs = kb[kcnt % 3]
                                qk_norm_rope(qkbufs[kcnt % 2], bank, nt, gk4, rcs[si], rss[si], kbs)
                                pbi = 6 + kcnt % 2
                                kcnt += 1

                                def fin(kbs=kbs, pbi=pbi, cb=cb, si=si, off=off, nt=nt, last=last_, first=first_,
                                        u0=u0, c_lo=c_lo, c_hi=c_hi):
                                    pb = psbf(pbi)
                                    for h in range(4):
                                        S.op(PE, lambda e: e.transpose(pb[:, h, :nt], kbs[:nt, h * 128:(h + 1) * 128],
                                                                       ident[:nt, :nt]),
                                             reads=[kbs.b, ident.b], writes=[psum[pbi].b], inc=(h == 3))
                                    ks = kst[cb % 2]
                                    S.op(ACT, lambda e: e.activation(out=ks[:, :, off:off + nt], in_=pb[:, 0:4, :nt],
                                                                     func=AF.Copy),
                                         reads=[psum[pbi].b], writes=[ks.b] if first else [],
                                         adds=[] if first else [ks.b])
                                    if last:
                                        S.dma(ACT, KT[cb * 4:(cb + 1) * 4, :, u0 + c_lo:u0 + c_hi].rearrange("h d t -> d h t"),
                                              ks[:, :, c_lo:c_hi], reads=[ks.b], adds=[kvb])
                                pend.append(fin)
                            else:
                                vbs = vb[cnt % 2]
                                S.op(ACT, lambda e: e.activation(out=vbs[:nt, :], in_=bank.t[:nt, :], func=AF.Copy),
                                     reads=[bank.b], writes=[vbs.b])
                                S.dma(ACT, V[u0 + off:u0 + off + nt, (cb - 3) * 512:(cb - 2) * 512], vbs[:nt, :],
                                      reads=[vbs.b], adds=[kvb])
                            while len(pend) > 2:
                                pend.pop(0)()
                            for fn_ in nsched.pop(ucount, []):
                                fn_()
                            ucount += 1
                    while pend:
                        pend.pop(0)()
                    for k_ in sorted(nsched):
                        for fn_ in nsched[k_]:
                            fn_()
                S.barrier(skip_sems=castpool[0])
        if 0 in cfg.sweeps:
            emit_casts(len(casts))


        wkv_es.close()
        es_r2 = ExitStack()
        for i in range(NW):
            wring.append(sb(es_r2, f"wr{i}", [128, 16, 512], BF16))

        SCALE = 1.0 / math.sqrt(HD)
        if 2 in cfg.sweeps:
            LQ["q"] = POOL
            with ExitStack() as s2:
                hTs = [sb(s2, f"hT2_{k}", [128, KC, 528], BF16) for k in range(2)]
                hTbs = [[Buf(f"hT2_{k}_{i}") for i in range(5)] for k in range(2)]

                def s2_norm(jj, bufs3):
                    a_, ntok_ = cfg.mix_tiles[jj]
                    return norm_stages(bufs3, lambda off, nt, a_=a_: xe[a_ - 8 + off:a_ - 8 + off + nt, :], ntok_ + 16,
                                       gb1, hTs[jj % 2], hTbs[jj % 2], 0)

                with ExitStack() as p0:
                    xt0 = [sb(p0, un("xt"), [128, D], F32) for i in range(2)]
                    hb0 = [sb(p0, un("hb"), [128, D], BF16) for i in range(2)]
                    ssb0 = [sb(p0, un("ssb"), [128, 2], F32) for i in range(2)]
                    for n1, n2 in s2_norm(0, (xt0, hb0, ssb0)):
                        n1()
                        n2()
                    cbarrier()
                attnT = sb(s2, "attnT", [128, 4, 512], BF16)
                pmT = sb(s2, "pmT", [128, 8, 512], BF16)
                wpl = sb(s2, "wpl", [128, 4, 2, 256], BF16)
                for g in range(4):
                    S.dma(POOL, wpl[:, g, :, :], wb_pool[g * 256:(g + 1) * 256, :].rearrange("(k p) c -> p k c", p=128),
                          reads=[wbuf["pool"]], adds=[wpl.b])
                for j, (a, ntok) in enumerate(cfg.mix_tiles):
                    sub = subtiles(ntok)
                    nh = ntok + 16
                    hT, hTb = hTs[j % 2], hTbs[j % 2]
                    with ExitStack() as pab:
                        qT = sb(pab, un("qT"), [128, 12, 512], BF16)
                        kw = [sb(pab, un("kw"), [128, 4, 512 + 128 * Dg], BF16) for Dg in DIL]
                        kvt = sb(pab, un("kvt"), [128, nkt], F32)
                        for g, Dg in enumerate(DIL):
                            S.dma(POOL, kw[g][:, :, 0:ntok + 128 * Dg],
                                  KT[4 * g:4 * g + 4, :, a - 64 * Dg:a + ntok + 64 * Dg].rearrange("h d t -> d h t"),
                                  reads=[kvb], writes=[kw[g].b])
                        S.dma(POOL, kvt[:, :], kvtab[j, :, :], writes=[kvt.b])
                        with ExitStack() as pa:
                            qkbufs = [(sb(pa, un("junk"), [128, 512], BF16), sb(pa, un("ss4"), [128, 8], F32),
                                       sb(pa, un("kn"), [128, 512], F32), sb(pa, un("A"), [128, 512], F32),
                                       sb(pa, un("B"), [128, 512], F32)) for i in range(2)]
                            rc = [sb(pa, un("rc"), [128, HD], F32) for i in range(4)]
                            rs_ = [sb(pa, un("rs"), [128, HD], F32) for i in range(4)]
                            qb = [sb(pa, un("qb"), [128, 512], BF16) for i in range(3)]
                            for si, (off, nt) in enumerate(sub):
                                load_rope(rc[si], rs_[si], a + off, nt)
                            cnt = 0
                            pend = []
                            for cb in range(3):
                                w = wload(wb_in, "in_rest", 0, KC, QOFF + cb * 512)
                                for si, (off, nt) in enumerate(sub):
                                    bank = psum[2 + (cnt % 4)]
                                    for kc in range(KC):
                                        S.op(PE, lambda e: e.matmul(bank.t[:nt, :], lhsT=hT[:, kc, 8 + off:8 + off + nt],
                                                                    rhs=w[:, kc, :], start=(kc == 0), stop=(kc == KC - 1)),
                                             reads=hTb + [w.b], writes=[bank.b], inc=(kc == KC - 1))
                                    qbs = qb[cnt % 3]
                                    qk_norm_rope(qkbufs[cnt % 2], bank, nt, gq4, rc[si], rs_[si], qbs, beng=POOL)
                                    pbi = 6 + cnt % 2
                                    cnt += 1

                                    def fin(qbs=qbs, pbi=pbi, cb=cb, off=off, nt=nt):
                                        pb = psbf(pbi)
                                        for h in range(4):
                                            S.op(PE, lambda e: e.transpose(pb[:, h, :nt], qbs[:nt, h * 128:(h + 1) * 128],
                                                                           ident[:nt, :nt]),
                                                 reads=[qbs.b, ident.b], writes=[psum[pbi].b], inc=(h == 3))
                                        S.op(ACT, lambda e: e.activation(out=qT[:, cb * 4:(cb + 1) * 4, off:off + nt],
                                                                         in_=pb[:, 0:4, :nt], func=AF.Copy),
                                             reads=[psum[pbi].b], adds=[qT.b])
                                    pend.append(fin)
                                    while len(pend) > 2:
                                        pend.pop(0)()
                            while pend:
                                pend.pop(0)()
                        cbarrier()
                        with ExitStack() as pbk:
                            NSB = 3
                            p1 = [sb(pbk, un("p1"), [128, 512], BF16) for i in range(NSB)]
                            p2 = [sb(pbk, un("p2"), [128, 512], BF16) for i in range(NSB)]
                            vt1 = [sb(pbk, un("vt1"), [128, 512], BF16) for i in range(NSB)]
                            vt2 = [sb(pbk, un("vt2"), [128, 512], BF16) for i in range(NSB)]
                            accO = sb(pbk, un("accO"), [128, 4, 512], F32)
                            accD = sb(pbk, un("accD"), [128, 4, 512], F32)
                            units = attn_units(ntok)

                            def sta(ui):
                                g, r, l0, nq = units[ui]
                                Dg = DIL[g]
                                c1 = r + Dg * l0
                                c2 = c1 + 128 * Dg
                                u1 = a - 64 * Dg + c1
                                u2 = a - 64 * Dg + c2
                                v1, v2 = vt1[ui % NSB], vt2[ui % NSB]
                                S.dma(POOL, v1[:, :], V[u1:u1 + Dg * 127 + 1:Dg, g * 512:(g + 1) * 512],
                                      reads=[kvb], writes=[v1.b])
                                S.dma(POOL, v2[:nq, :], V[u2:u2 + Dg * (nq - 1) + 1:Dg, g * 512:(g + 1) * 512],
                                      reads=[kvb], writes=[v2.b])
                                qs = slice(c1, c1 + Dg * (nq - 1) + 1, Dg)
                                bS1, bS2 = psum[2 * (ui % 2)], psum[2 * (ui % 2) + 1]
                                P1, P2 = p1[ui % NSB], p2[ui % NSB]
                                bS1v = bS1.t[:, 0:4 * nq].rearrange("p (h i) -> p h i", h=4)
                                bS2v = bS2.t[:nq, 0:4 * nq].rearrange("p (h i) -> p h i", h=4)
                                S.op(PE, lambda e: e.matmul(bS1v, lhsT=ident[:, :], rhs=negm[:, 0, :, 0:nq],
                                                            start=True, stop=False),
                                     reads=[ident.b, negm.b], writes=[bS1.b], inc=False)
                                for h in range(4):
                                    S.op(PE, lambda e: e.matmul(bS1.t[:, h * nq:(h + 1) * nq],
                                                                lhsT=kw[g][:, h, c1:c1 + Dg * 127 + 1:Dg],
                                                                rhs=qT[:, 4 * g + h, qs], start=False, stop=(h == 3)),
                                         reads=[kw[g].b, qT.b], adds=[bS1.b], inc=(h == 3))
                                S.op(PE, lambda e: e.matmul(bS2v, lhsT=ident[:nq, :nq], rhs=negm[:nq, 1, :, 0:nq],
                                                            start=True, stop=False),
                                     reads=[ident.b, negm.b], writes=[bS2.b], inc=False)
                                for h in range(4):
                                    S.op(PE, lambda e: e.matmul(bS2.t[:nq, h * nq:(h + 1) * nq],
                                                                lhsT=kw[g][:, h, c2:c2 + Dg * (nq - 1) + 1:Dg],
                                                                rhs=qT[:, 4 * g + h, qs], start=False, stop=(h == 3)),
                                         reads=[kw[g].b, qT.b], adds=[bS2.b], inc=(h == 3))
                                S.op(ACT, lambda e: e.activation(out=P1[:, :4 * nq], in_=bS1.t[:, 0:4 * nq], func=AF.Exp,
                                                                 bias=kvt[:, 2 * ui:2 * ui + 1], scale=SCALE),
                                     reads=[bS1.b, kvt.b], writes=[P1.b])
                                S.op(ACT, lambda e: e.activation(out=P2[:nq, :4 * nq], in_=bS2.t[:nq, 0:4 * nq], func=AF.Exp,
                                                                 bias=kvt[:nq, 2 * ui + 1:2 * ui + 2], scale=SCALE),
                                     reads=[bS2.b, kvt.b], writes=[P2.b])

                            def stb(ui):
                                g, r, l0, nq = units[ui]
                                Dg = DIL[g]
                                c1 = r + Dg * l0
                                qs = slice(c1, c1 + Dg * (nq - 1) + 1, Dg)
                                v1, v2 = vt1[ui % NSB], vt2[ui % NSB]
                                P1, P2 = p1[ui % NSB], p2[ui % NSB]
                                bO, bD = psum[4 + 2 * (ui % 2)], psum[5 + 2 * (ui % 2)]
                                for h in range(4):
                                    S.op(PE, lambda e: e.matmul(bO.t[:, h * nq:(h + 1) * nq], lhsT=v1[:, h * 128:(h + 1) * 128],
                                                                rhs=P1[:, h * nq:(h + 1) * nq], start=True, stop=False),
                                         reads=[v1.b, P1.b], writes=[bO.b] if h == 0 else [],
                                         adds=[] if h == 0 else [bO.b], inc=False)
                                    S.op(PE, lambda e: e.matmul(bO.t[:, h * nq:(h + 1) * nq], lhsT=v2[:nq, h * 128:(h + 1) * 128],
                                                                rhs=P2[:nq, h * nq:(h + 1) * nq], start=False, stop=True),
                                         reads=[v2.b, P2.b], adds=[bO.b], inc=(h == 3))
                                S.op(PE, lambda e: e.matmul(bD.t[:, 0:4 * nq], lhsT=ones[:, :], rhs=P1[:, 0:4 * nq],
                                                            start=True, stop=False),
                                     reads=[ones.b, P1.b], writes=[bD.b], inc=False)
                                S.op(PE, lambda e: e.matmul(bD.t[:, 0:4 * nq], lhsT=ones[:nq, :], rhs=P2[:nq, 0:4 * nq],
                                                            start=False, stop=True),
                                     reads=[ones.b, P2.b], adds=[bD.b])
                                bOv = bO.t[:, 0:4 * nq].rearrange("p (h i) -> p h i", h=4)
                                bDv = bD.t[:, 0:4 * nq].rearrange("p (h i) -> p h i", h=4)
                                if g == 0:
                                    S.op(ACT, lambda e: e.activation(out=accO[:, :, qs], in_=bOv, func=AF.Copy),
                                         reads=[bO.b], adds=[accO.b])
                                    S.op(DVE, lambda e: e.tensor_copy(out=accD[:, :, qs], in_=bDv),
                                         reads=[bD.b], adds=[accD.b])
                                else:
                                    S.op(DVE, lambda e: e.tensor_tensor(out=accO[:, :, qs], in0=accO[:, :, qs], in1=bOv,
                                                                        op=ALU.add), reads=[bO.b], writes=[accO.b])
                                    S.op(DVE, lambda e: e.tensor_tensor(out=accD[:, :, qs], in0=accD[:, :, qs], in1=bDv,
                                                                        op=ALU.add), reads=[bD.b], writes=[accD.b])

                            for t in range(len(units) + 1):
                                if t < len(units):
                                    sta(t)
                                if t >= 1:
                                    stb(t - 1)
                            S.op(DVE, lambda e: e.reciprocal(out=accD[:, :, :ntok], in_=accD[:, :, :ntok]),
                                 writes=[accD.b])
                            S.op(DVE, lambda e: e.scalar_tensor_tensor(out=attnT[:, :, :ntok], in0=accO[:, :, :ntok],
                                                                       scalar=0.5, in1=accD[:, :, :ntok],
                                                                       op0=ALU.mult, op1=ALU.mult),
                                 reads=[accO.b, accD.b], writes=[attnT.b])
                        cbarrier()
                    with ExitStack() as pc:
                        up = sb(pc, un("up"), [128, 8, 528], F32)
                        upb = [Buf(f"up{g}") for g in range(4)]
                        tbs = [[sb(pc, un("Pa"), [128, 2, 528], F32), sb(pc, un("Pb"), [128, 2, 528], F32)]
                               for i in range(2)]
                        ic = sb(pc, un("ic"), [128, 4, 512], F32)
                        diffT = sb(pc, un("diffT"), [128, 8, 512], BF16)
                        dfb = [Buf(f"df{g}") for g in range(4)]
                        for g in range(4):
                            S.dma(POOL, ic[:, g, :ntok], invc[g:g + 1, a:a + ntok].partition_broadcast(128), adds=[ic.b])
                        splits = [(n0, min(512, nh - n0)) for n0 in range(0, nh, 512)]
                        cnt = 0

                        def pool_group(g, eng, tb):
                            R = POOL_R[g]
                            uv = up[:, 2 * g:2 * g + 2, :]
                            cur, ln, m = uv, nh, 1
                            curb = upb[g]
                            ti = 0
                            while m < 2 * R:
                                nxt = tb[ti]
                                ti ^= 1
                                S.op(eng, lambda e: e.tensor_tensor(out=nxt[:, :, 0:ln - m], in0=cur[:, :, 0:ln - m],
                                                                    in1=cur[:, :, m:ln], op=ALU.add),
                                     reads=[curb], writes=[nxt.b])
                                cur, curb = nxt.t, nxt.b
                                ln -= m
                                m *= 2
                            Wt = tb[ti]
                            S.op(eng, lambda e: e.tensor_tensor(out=Wt[:, :, 0:ntok], in0=cur[:, :, 8 - R:8 - R + ntok],
                                                                in1=uv[:, :, 8 + R:8 + R + ntok], op=ALU.add),
                                 reads=[curb, upb[g]], writes=[Wt.b])
                            Tt = tb[ti ^ 1]
                            S.op(eng, lambda e: e.tensor_tensor(out=Tt[:, :, 0:ntok], in0=Wt[:, :, 0:ntok],
                                                                in1=ic[:, g:g + 1, :ntok].to_broadcast([128, 2, ntok]),
                                                                op=ALU.mult),
                                 reads=[Wt.b, ic.b], writes=[Tt.b])
                            S.op(eng, lambda e: e.tensor_tensor(out=diffT[:, 2 * g:2 * g + 2, :ntok], in0=Tt[:, :, 0:ntok],
                                                                in1=uv[:, :, 8:8 + ntok], op=ALU.subtract),
                                 reads=[Tt.b, upb[g]], writes=[dfb[g]])

                        def pool_mm(g):
                            nonlocal_cnt = [0]
                            for oc in range(2):
                                bank = psum[4 + (2 * g + oc) % 4]
                                for kc in range(2):
                                    S.op(PE, lambda e: e.matmul(bank.t[:, 0:ntok], lhsT=wpl[:, g, kc, oc * 128:(oc + 1) * 128],
                                                                rhs=diffT[:, 2 * g + kc, :ntok], start=(kc == 0),
                                                                stop=(kc == 1)),
                                         reads=[wpl.b, dfb[g]], writes=[bank.b], inc=(kc == 1))
                                S.op(ACT, lambda e: e.activation(out=pmT[:, 2 * g + oc, :ntok], in_=bank.t[:, 0:ntok],
                                                                 func=AF.Copy, scale=psh[:, 2 * g + oc:2 * g + oc + 1]),
                                     reads=[bank.b, psh.b], adds=[pmT.b])

                        for blk in range(2):
                            w = wload(wb_in, "in_rest", 0, KC, blk * 512)
                            for ci in range(4):
                                c = blk * 4 + ci
                                for (n0, nn) in splits:
                                    bank = psum[cnt % 4]
                                    cnt += 1
                                    for kc in range(KC):
                                        S.op(PE, lambda e: e.matmul(bank.t[:, 0:nn], lhsT=w[:, kc, ci * 128:(ci + 1) * 128],
                                                                    rhs=hT[:, kc, n0:n0 + nn], start=(kc == 0),
                                                                    stop=(kc == KC - 1)),
                                             reads=hTb + [w.b], writes=[bank.b], inc=(kc == KC - 1))
                                    S.op(ACT, lambda e: e.activation(out=up[:, c, n0:n0 + nn], in_=bank.t[:, 0:nn],
                                                                     func=AF.Copy), reads=[bank.b], adds=[upb[c // 2]])
                            if blk == 0:
                                pool_group(0, POOL, tbs[0])
                                pool_group(1, DVE, tbs[1])
                            else:
                                pool_mm(0)
                                pool_mm(1)
                                pool_group(2, POOL, tbs[0])
                                pool_group(3, DVE, tbs[1])
                                pool_mm(3)
                                pool_mm(2)
                    cbarrier()
                    pde = ExitStack()
                    mT = sb(pde, un("mT"), [128, KC, 512], BF16)
                    nsched = {}
                    if j + 1 < len(cfg.mix_tiles):
                        xtp = [sb(pde, un("xt"), [128, D], F32) for i in range(2)]
                        hbp = [sb(pde, un("hb"), [128, D], BF16) for i in range(2)]
                        ssbp = [sb(pde, un("ssb"), [128, 2], F32) for i in range(2)]
                        for k, (n1, n2) in enumerate(s2_norm(j + 1, (xtp, hbp, ssbp))):
                            nsched.setdefault(1 + 2 * k, []).append(n1)
                            nsched.setdefault(2 + 2 * k, []).append(n2)
                    dstep = 0
                    xin = [sb(pde, un("xin"), [128, 512], F32) for i in range(2)]
                    xout = [sb(pde, un("xout"), [128, 512], F32) for i in range(2)]
                    with ExitStack() as pd:
                        t0 = [sb(pd, un("t0"), [128, 512], F32) for i in range(4)]
                        t1 = [sb(pd, un("t1"), [128, 512], F32) for i in range(4)]
                        m0 = [sb(pd, un("m0"), [128, 512], F32) for i in range(2)]
                        m1 = [sb(pd, un("m1"), [128, 512], F32) for i in range(2)]
                        cnt = 0
                        for grp in range(4):
                            for br, tt in ((0, t0), (1, t1)):
                                for fn_ in nsched.pop(dstep, []):
                                    fn_()
                                dstep += 1
                                w = wload(wb_in, "in_rest", 0, KC, GOFF + br * D + grp * 512)
                                for ci in range(4):
                                    c = grp * 4 + ci
                                    bank = psum[cnt % 8]
                                    cnt += 1
                                    for kc in range(KC):
                                        S.op(PE, lambda e: e.matmul(bank.t[:, 0:ntok], lhsT=w[:, kc, ci * 128:(ci + 1) * 128],
                                                                    rhs=hT[:, kc, 8:8 + ntok], start=(kc == 0),
                                                                    stop=(kc == KC - 1)),
                                             reads=hTb + [w.b], writes=[bank.b], inc=(kc == KC - 1))
                                    S.op(ACT, lambda e: e.activation(out=tt[ci][:, :ntok], in_=bank.t[:, 0:ntok], func=AF.Tanh,
                                                                     bias=bgh[:, br * 16 + c:br * 16 + c + 1], scale=0.5),
                                         reads=[bank.b, bgh.b], writes=[tt[ci].b])
                            for fn_ in nsched.pop(dstep, []):
                                fn_()
                            dstep += 1
                            wpa = wload(wb_po, "po", 0, 8, grp * 512)
                            wload_into(wpa, 8, wb_ao, "ao", 0, 4, grp * 512)
                            for ci in range(4):
                                c = grp * 4 + ci
                                bP, bA = psum[cnt % 8], psum[(cnt + 1) % 8]
                                cnt += 2
                                for kc in range(8):
                                    S.op(PE, lambda e: e.matmul(bP.t[:, 0:ntok], lhsT=wpa[:, kc, ci * 128:(ci + 1) * 128],
                                                                rhs=pmT[:, kc, :ntok], start=(kc == 0), stop=(kc == 7)),
                                         reads=[wpa.b, pmT.b], writes=[bP.b], inc=(kc == 7))
                                for kc in range(4):
                                    S.op(PE, lambda e: e.matmul(bA.t[:, 0:ntok], lhsT=wpa[:, 8 + kc, ci * 128:(ci + 1) * 128],
                                                                rhs=attnT[:, kc, :ntok], start=(kc == 0), stop=(kc == 3)),
                                         reads=[wpa.b, attnT.b], writes=[bA.b], inc=(kc == 3))
                                M0, M1 = m0[ci % 2], m1[ci % 2]
                                S.op(DVE, lambda e: e.scalar_tensor_tensor(out=M0[:, :ntok], in0=t0[ci][:, :ntok], scalar=1.0,
                                                                           in1=bP.t[:, 0:ntok], op0=ALU.add, op1=ALU.mult),
                                     reads=[t0[ci].b, bP.b], writes=[M0.b])
                                S.op(DVE, lambda e: e.scalar_tensor_tensor(out=M1[:, :ntok], in0=t1[ci][:, :ntok], scalar=1.0,
                                                                           in1=bA.t[:, 0:ntok], op0=ALU.add, op1=ALU.mult),
                                     reads=[t1[ci].b, bA.b], writes=[M1.b])
                                S.op(POOL, lambda e: e.tensor_tensor(out=mT[:, c, :ntok], in0=M0[:, :ntok], in1=M1[:, :ntok],
                                                                     op=ALU.add), reads=[M0.b, M1.b], adds=[mT.b])
                    for k_ in sorted(nsched):
                        for fn_ in nsched[k_]:
                            fn_()
                    with ExitStack() as pe_:
                        cnt = 0
                        for cb in range(4):
                            w = wload(wb_o, "o", 0, KC, cb * 512)
                            for si, (off, nt) in enumerate(sub):
                                bank = psum[cnt % 4]
                                XI, XO = xin[cnt % 2], xout[cnt % 2]
                                cnt += 1
                                S.dma(POOL, XI[:nt, :], xe[a + off:a + off + nt, cb * 512:(cb + 1) * 512], writes=[XI.b])
                                for kc in range(KC):
                                    S.op(PE, lambda e: e.matmul(bank.t[:nt, :], lhsT=mT[:, kc, off:off + nt], rhs=w[:, kc, :],
                                                                start=(kc == 0), stop=(kc == KC - 1)),
                                         reads=[mT.b, w.b], writes=[bank.b], inc=(kc == KC - 1))
                                S.op(DVE, lambda e: e.tensor_tensor(out=XO[:nt, :], in0=bank.t[:nt, :], in1=XI[:nt, :],
                                                                    op=ALU.add), reads=[bank.b, XI.b], writes=[XO.b])
                                m_ = a - 1024 + off
                                S.dma(ACT, X1[m_:m_ + nt, cb * 512:(cb + 1) * 512], XO[:nt, :], reads=[XO.b], adds=[x1b[j]])
                    cbarrier()
                    pde.close()
                S.barrier()

        es_r2.close()
        es_g1.close()
        del wring[:]
        for i in range(NW):
            wring.append(sb(es, f"wr3_{i}", [128, 16, 512], BF16))
        if 3 in cfg.sweeps:
            LQ["q"] = POOL
            with ExitStack() as s3:
                gb2 = sb(s3, "gb2", [128, D], F32)
                S.dma(POOL, gb2[:, :], g2[0:1, :].partition_broadcast(128), writes=[gb2.b])
                xt = [sb(s3, "xt3_%d" % i, [128, D], F32) for i in range(2)]
                hb = [sb(s3, "hb3_%d" % i, [128, D], BF16) for i in range(2)]
                ssb = [sb(s3, "ssb3_%d" % i, [128, 2], F32) for i in range(2)]
                h2T2 = [sb(s3, f"h2T{i}", [128, KC, 514], BF16) for i in range(2)]
                h2Tb2 = [[Buf(f"h2T{k}_{i}") for i in range(5)] for k in range(2)]
                tv = sb(s3, "tv", [128, 514], F32)
                aT = sb(s3, "aT", [128, FC, 512], BF16)
                NS3 = 3
                Gs = [sb(s3, "Gs%d" % i, [128, 514], F32) for i in range(NS3)]
                cA = [sb(s3, "cA%d" % i, [128, 512], F32) for i in range(NS3)]
                Vs = [sb(s3, "Vs%d" % i, [128, 512], F32) for i in range(NS3)]
                bX = [sb(s3, "bX%d" % i, [128, 512], F32) for i in range(NS3)]
                bY = [sb(s3, "bY%d" % i, [128, 512], F32) for i in range(NS3)]
                xin = [sb(s3, "xin3_%d" % i, [128, 512], F32) for i in range(2)]
                xout = [sb(s3, "xout3_%d" % i, [128, 512], F32) for i in range(2)]
                GC = 0.7978845608028654
                def s3_norm(j_):
                    mm0 = cfg.ffn_tiles[j_]
                    return norm_stages((xt, hb, ssb), lambda off, nt, mm0=mm0: X1[mm0 + off:mm0 + off + nt, :], 514,
                                       gb2, h2T2[j_ % 2], h2Tb2[j_ % 2], 0, src_reads=x1_deps(mm0, 514))

                for n1, n2 in s3_norm(0):
                    n1()
                    n2()
                for j, m0_ in enumerate(cfg.ffn_tiles):
                    h2T, h2Tb = h2T2[j % 2], h2Tb2[j % 2]
                    nsched = {}
                    if j + 1 < len(cfg.ffn_tiles):
                        for k, (n1, n2) in enumerate(s3_norm(j + 1)):
                            nsched.setdefault(1 + 2 * k, []).append(n1)
                            nsched.setdefault(2 + 2 * k, []).append(n2)
                    S.dma(POOL, tv[:, :], tokv[0:1, 1024 + m0_:1024 + m0_ + 514].partition_broadcast(128), writes=[tv.b])
                    wts = {}

                    def st0(f):
                        fb, fi = divmod(f, 4)
                        if fi == 0:
                            wts[fb] = (wload(wb_up, "up", 0, KC, fb * 512), wload(wb_up, "up", 0, KC, DFF + fb * 512))
                        wg, wv = wts[fb]
                        s_ = f % 2
                        bA, bB, bV = psum[3 * s_], psum[3 * s_ + 1], psum[3 * s_ + 2]
                        for kc in range(KC):
                            S.op(PE, lambda e: e.matmul(bA.t[:, 0:512], lhsT=wg[:, kc, fi * 128:(fi + 1) * 128],
                                                        rhs=h2T[:, kc, 0:512], start=(kc == 0), stop=(kc == KC - 1)),
                                 reads=h2Tb + [wg.b], writes=[bA.b], inc=(kc == KC - 1))
                        for kc in range(KC):
                            S.op(PE, lambda e: e.matmul(bB.t[:, 0:2], lhsT=wg[:, kc, fi * 128:(fi + 1) * 128],
                                                        rhs=h2T[:, kc, 512:514], start=(kc == 0), stop=(kc == KC - 1)),
                                 reads=h2Tb + [wg.b], writes=[bB.b], inc=(kc == KC - 1))
                        for kc in range(KC):
                            S.op(PE, lambda e: e.matmul(bV.t[:, 0:512], lhsT=wv[:, kc, fi * 128:(fi + 1) * 128],
                                                        rhs=h2T[:, kc, 1:513], start=(kc == 0), stop=(kc == KC - 1)),
                                 reads=h2Tb + [wv.b], writes=[bV.b], inc=(kc == KC - 1))

                    def st1(f):
                        s_ = f % 2
                        bA, bB, bV = psum[3 * s_], psum[3 * s_ + 1], psum[3 * s_ + 2]
                        G, CA, VS, X = Gs[f % NS3], cA[f % NS3], Vs[f % NS3], bX[f % NS3]
                        S.op(ACT, lambda e: e.activation(out=VS[:, :], in_=bV.t[:, 0:512], func=AF.Copy),
                             reads=[bV.b], writes=[VS.b])
                        S.op(DVE, lambda e: e.tensor_tensor(out=G[:, 0:512], in0=bA.t[:, 0:512], in1=tv[:, 0:512],
                                                            op=ALU.mult), reads=[bA.b, tv.b], writes=[G.b])
                        S.op(DVE, lambda e: e.tensor_tensor(out=G[:, 512:514], in0=bB.t[:, 0:2], in1=tv[:, 512:514],
                                                            op=ALU.mult), reads=[bB.b, tv.b], adds=[G.b])
                        S.op(DVE, lambda e: e.tensor_scalar(out=CA[:, :], in0=G[:, 1:513], scalar1=cw[:, 1, f:f + 1],
                                                            scalar2=cbv[:, f:f + 1], op0=ALU.mult, op1=ALU.add),
                             reads=[G.b, cw.b, cbv.b], writes=[CA.b])
                        S.op(DVE, lambda e: e.scalar_tensor_tensor(out=X[:, :], in0=G[:, 0:512], scalar=cw[:, 0, f:f + 1],
                                                                   in1=CA[:, :], op0=ALU.mult, op1=ALU.add),
                             reads=[G.b, cw.b, CA.b], writes=[X.b])
                        S.op(DVE, lambda e: e.scalar_tensor_tensor(out=CA[:, :], in0=G[:, 2:514], scalar=cw[:, 2, f:f + 1],
                                                                   in1=X[:, :], op0=ALU.mult, op1=ALU.add),
                             reads=[G.b, cw.b, X.b], writes=[CA.b])
                        S.op(POOL, lambda e: e.tensor_tensor(out=X[:, :], in0=CA[:, :], in1=CA[:, :], op=ALU.mult),
                             reads=[CA.b], writes=[X.b])

                    def st2(f):
                        G, CA, VS, X, Y = Gs[f % NS3], cA[f % NS3], Vs[f % NS3], bX[f % NS3], bY[f % NS3]
                        S.op(DVE, lambda e: e.tensor_scalar(out=Y[:, :], in0=X[:, :], scalar1=0.044715, scalar2=1.0,
                                                            op0=ALU.mult, op1=ALU.add), reads=[X.b], writes=[Y.b])
                        S.op(POOL, lambda e: e.tensor_tensor(out=X[:, :], in0=Y[:, :], in1=CA[:, :], op=ALU.mult),
                             reads=[Y.b, CA.b], writes=[X.b])
                        S.op(ACT, lambda e: e.activation(out=Y[:, :], in_=X[:, :], func=AF.Tanh, scale=GC),
                             reads=[X.b], writes=[Y.b])
                        S.op(DVE, lambda e: e.scalar_tensor_tensor(out=G[:, 0:512], in0=Y[:, :], scalar=1.0, in1=CA[:, :],
                                                                   op0=ALU.add, op1=ALU.mult),
                             reads=[Y.b, CA.b], writes=[G.b])
                        S.op(DVE, lambda e: e.scalar_tensor_tensor(out=aT[:, f, :], in0=G[:, 0:512], scalar=0.5,
                                                                   in1=VS[:, :], op0=ALU.mult, op1=ALU.mult),
                             reads=[G.b, VS.b], adds=[aT.b])

                    for t in range(FC + 2):
                        if t < FC:
                            st0(t)
                        if 0 <= t - 1 < FC:
                            st1(t - 1)
                        if 0 <= t - 2 < FC:
                            st2(t - 2)
                    sub = subtiles(512)
                    kblocks = [(0, 16), (16, 16), (32, 12)]
                    wstep = 0
                    for cb in range(4):
                        banks = [psum[4 + si] for si in range(4)]
                        for kb, (k0, nk) in enumerate(kblocks):
                            for fn_ in nsched.pop(wstep, []):
                                fn_()
                            wstep += 1
                            wd = wload(wb_down, "down", k0, nk, cb * 512)
                            for si, (off, nt) in enumerate(sub):
                                for kk in range(nk):
                                    kf_ = k0 + kk
                                    first, last = (kf_ == 0), (kf_ == FC - 1)
                                    S.op(PE, lambda e: e.matmul(banks[si].t[:nt, :], lhsT=aT[:, kf_, off:off + nt],
                                                                rhs=wd[:, kk, :], start=first, stop=last),
                                         reads=[aT.b, wd.b], writes=[banks[si].b] if first else [],
                                         adds=[] if first else [banks[si].b], inc=(kk == nk - 1))
                        if cb == 3:
                            for k_ in sorted(nsched):
                                for fn_ in nsched[k_]:
                                    fn_()
                            nsched = {}
                        for si, (off, nt) in enumerate(sub):
                            XI, XO = xin[si % 2], xout[si % 2]
                            r0 = m0_ + 1 + off
                            S.dma(POOL, XI[:nt, :], X1[r0:r0 + nt, cb * 512:(cb + 1) * 512],
                                  reads=x1_deps(m0_, 514), writes=[XI.b])
                            S.op(DVE, lambda e: e.tensor_tensor(out=XO[:nt, :], in0=banks[si].t[:nt, :], in1=XI[:nt, :],
                                                                op=ALU.add), reads=[banks[si].b, XI.b], writes=[XO.b])
                            S.dma(ACT, y[m0_ + off:m0_ + off + nt, cb * 512:(cb + 1) * 512], XO[:nt, :], reads=[XO.b])
                S.barrier()

        if 3 not in cfg.sweeps:
            zt = sb(es, "zt", [128, D], F32)
            S.op(DVE, lambda e: e.memset(zt[:, :], 0.0), writes=[zt.b])
            for r0 in range(0, NTC, 128):
                S.dma(ACT, y[r0:r0 + 128, :], zt[:, :], reads=[zt.b])

        S.barrier()
    return nc


def host_tables(cfg, seq_start, seq_len):
    NE = cfg.NE
    pos = seq_start - HALO + np.arange(NE)
    valid = ((pos >= 0) & (pos < seq_len))
    half = HD // 2
    inv_freq = (ROPE_THETA ** (-np.arange(half, dtype=np.float32) / half)).astype(np.float32)
    ang = pos.astype(np.float32)[:, None] * inv_freq[None, :]
    cos = np.cos(ang).astype(np.float32)
    sin = np.sin(ang).astype(np.float32)
    ropeC = np.concatenate([cos, cos], axis=1)
    ropeS = np.concatenate([-sin, sin], axis=1)
    invcnt = np.ones((4, NE), np.float32)
    for g, R in enumerate(POOL_R):
        lo = np.clip(pos - R, 0, seq_len)
        hi = np.clip(pos + R + 1, 0, seq_len)
        cnt = np.maximum(hi - lo, 1)
        invcnt[g] = (1.0 / cnt).astype(np.float32)
    nkt = max(len(attn_units(n)) for _, n in cfg.mix_tiles) * 2
    kvtab = np.zeros((len(cfg.mix_tiles), 128, nkt), np.float32)
    for j, (a, ntok) in enumerate(cfg.mix_tiles):
        for ui, (g, r, l0, nq) in enumerate(attn_units(ntok)):
            Dg = DIL[g]
            base = a - 64 * Dg + r + Dg * l0
            for t, (b0, nk) in enumerate(((base, 128), (base + 128 * Dg, nq))):
                u = b0 + Dg * np.arange(nk)
                ok = (u >= 0) & (u < NE)
                vv = np.zeros(nk, np.float32)
                vv[ok] = valid[u[ok]]
                kvtab[j, :nk, 2 * ui + t] = (vv - 1.0) * 30000.0
    return dict(ropeC=ropeC, ropeS=ropeS, tokvalid=valid.astype(np.float32)[None, :], invcnt=invcnt, kvtab=kvtab)


def host_consts():
    p = np.arange(128)
    m1 = (p[None, :] <= p[:, None]).astype(np.float32)
    m2 = (p[:, None] <= p[None, :]).astype(np.float32)
    negm = np.stack([np.repeat(((m1 - 1.0) * 30000.0)[:, None, :], 4, axis=1),
                     np.repeat(((m2 - 1.0) * 30000.0)[:, None, :], 4, axis=1)], axis=1).reshape(128, 1024)
    return dict(ident=np.eye(128).astype(ml_dtypes.bfloat16), ones=np.ones((128, 128), ml_dtypes.bfloat16),
                mask12=np.concatenate([m1, m2], axis=1), negm=negm.astype(ml_dtypes.bfloat16))


def make_in_maps(cfg, inputs, seqs):
    f = lambda a: np.ascontiguousarray(np.asarray(a, dtype=np.float32))
    shared = dict(
        w_in=f(inputs["w_in"][0]), w_pool=f(inputs["w_pool"][0]).reshape(POOLW, 256), w_po=f(inputs["w_pool_out"][0]),
        w_ao=f(inputs["w_attn_out"][0]), w_o=f(inputs["w_o"][0]), w_up=f(inputs["w_up"][0]),
        w_down=f(inputs["w_down"][0]), g1=f(inputs["mix_norm_g"]), g2=f(inputs["ffn_norm_g"]),
        gq=f(inputs["q_norm_g"]), gk=f(inputs["k_norm_g"]), b_gate=f(inputs["b_gate"][0]),
        pscale=f(inputs["pool_scale"][0]), conv_w=f(inputs["conv_w"][0]), conv_b=f(inputs["conv_b"][0]),
        **host_consts())
    maps = []
    for xs, start in seqs:
        L = xs.shape[0]
        xe = np.zeros((cfg.NE, D), np.float32)
        lo = start - HALO
        a, b = max(lo, 0), min(lo + cfg.NE, L)
        xe[a - lo:b - lo] = xs[a:b]
        m = dict(shared)
        m["xe"] = xe
        m.update(host_tables(cfg, start, L))
        maps.append(m)
    return maps


_CACHE = {}


def kernel(**inputs):
    cfg = Cfg(4096)
    xp = np.asarray(inputs["x_prompt"], dtype=np.float32)
    xs = np.asarray(inputs["x_sample"], dtype=np.float32)
    seqs = [(xp[b], 0) for b in range(4)] + [(xs[0], 4096 * c) for c in range(4)]
    if "nc" not in _CACHE:
        _CACHE["nc"] = build(cfg)
    in_maps = make_in_maps(cfg, inputs, seqs)
    res = run_bass_kernel_spmd(_CACHE["nc"], in_maps, core_ids=list(range(NCORES)))
    ys = [np.asarray(r["y"], dtype=np.float32) for r in res.results]
    y_prompt = np.stack(ys[:4], axis=0)
    y_sample = np.concatenate(ys[4:], axis=0)[None]
    return (y_prompt, y_sample)
```

```python
import math
from contextlib import ExitStack

import numpy as np
import ml_dtypes

import concourse.bass as bass
import concourse.mybir as mybir
from concourse.bass_utils import run_bass_kernel_spmd

F32 = mybir.dt.float32
BF16 = mybir.dt.bfloat16
AF = mybir.ActivationFunctionType
ALU = mybir.AluOpType
AX = mybir.AxisListType

NCORES = 8
D = 2048
KC = 16
HD = 128
POOLW = 1024
AW = 1536
INW = 9728
DFF = 5632
FC = 44
QOFF, KOFF, VOFF, GOFF = 1024, 2560, 4096, 5632
HALO = 1025
EPS = 1e-6
DIL = (1, 4, 16)
POOL_R = (1, 2, 4, 8)
ROPE_THETA = 10000.0


class Cfg:
    def __init__(self, ntc=4096, debug=False, sweeps=(0, 1, 2, 3)):
        assert ntc % 512 == 0
        self.NTC = ntc
        self.NMIX = ntc + 2
        self.NE = ntc + 2050
        self.debug = debug
        self.sweeps = sweeps
        self.s1_tiles = [(u0, min(512, self.NE - u0)) for u0 in range(0, self.NE, 512)]
        MT = 464
        sizes = []
        rem = self.NMIX
        while rem > 642:
            sizes.append(512)
            rem -= 512
        if rem > 512:
            sizes.append(384)
            rem -= 384
        sizes.append(rem)
        starts = [1024 + sum(sizes[:i]) for i in range(len(sizes))]
        self.mix_tiles = list(zip(starts, sizes))
        self.ffn_tiles = [512 * j for j in range(ntc // 512)]


def subtiles(n):
    return [(o, min(128, n - o)) for o in range(0, n, 128)]


def attn_units(ntok):
    out = []
    for g, Dg in enumerate(DIL):
        for r in range(min(Dg, ntok)):
            nl = (ntok - r + Dg - 1) // Dg
            for l0 in range(0, nl, 128):
                out.append((g, r, l0, min(128, nl - l0)))
    return out


class Buf:
    __slots__ = ("name", "w", "r")

    def __init__(self, name=""):
        self.name = name
        self.w = {}
        self.r = {}


class Eng:
    def __init__(self, be, name, sem, key):
        self.be = be
        self.name = name
        self.sem = sem
        self.key = key
        self.cnt = 0
        self.pend = False
        self.waited = {}
        self.pool = []
        self.rr = 0


def _merge(d, s):
    for k, v in s.items():
        if d.get(k, 0) < v:
            d[k] = v


class Sched:
    def __init__(self, nc, es, ndma=(12, 12, 8)):
        self.nc = nc
        self.semh = {}
        self.dcnt = {}
        nk = [0]

        def newsem(name):
            k = nk[0]
            nk[0] += 1
            self.semh[k] = es.enter_context(nc.semaphore(name))
            return k

        self.pe = Eng(nc.tensor, "pe", None, newsem("s_pe"))
        self.act = Eng(nc.scalar, "act", None, newsem("s_act"))
        self.dve = Eng(nc.vector, "dve", None, newsem("s_dve"))
        self.pool = Eng(nc.gpsimd, "pool", None, newsem("s_pool"))
        self.sp = Eng(nc.sync, "sp", None, newsem("s_sp"))
        self.engs = [self.pe, self.act, self.dve, self.pool, self.sp]
        for e in self.engs:
            e.sem = self.semh[e.key]
        for e, n in ((self.sp, ndma[0]), (self.pool, ndma[1]), (self.act, ndma[2])):
            for i in range(n):
                k = newsem(f"d_{e.name}{i}")
                e.pool.append(k)
                self.dcnt[k] = 0

    def _wait(self, eng, deps):
        for k, v in deps.items():
            if k == eng.key and eng is self.pe:
                continue
            if eng.waited.get(k, 0) < v:
                eng.be.wait_ge(self.semh[k], v)
                eng.waited[k] = v

    @staticmethod
    def _deps(reads, writes, adds):
        deps = {}
        for b in reads:
            _merge(deps, b.w)
        for b in writes:
            _merge(deps, b.w)
            _merge(deps, b.r)
        for b in adds:
            _merge(deps, b.r)
        return deps

    @staticmethod
    def _commit(tokk, tokv, reads, writes, adds):
        for b in reads:
            if b.r.get(tokk, 0) < tokv:
                b.r[tokk] = tokv
        for b in writes:
            b.w = {tokk: tokv}
            b.r = {}
        for b in adds:
            if b.w.get(tokk, 0) < tokv:
                b.w[tokk] = tokv

    def op(self, eng, fn, reads=(), writes=(), adds=(), inc=True):
        self._wait(eng, self._deps(reads, writes, adds))
        ins = fn(eng.be)
        if inc:
            eng.cnt += 1
            ins.then_inc(eng.sem, 1)
            tokv = eng.cnt
        else:
            tokv = eng.cnt + 1
        self._commit(eng.key, tokv, reads, writes, adds)
        return ins

    def dma(self, q, out, in_, reads=(), writes=(), adds=(), nc_ok=False, pool=None):
        self._wait(q, self._deps(reads, writes, adds))
        if pool is None:
            k = q.pool[q.rr]
            q.rr = (q.rr + 1) % len(q.pool)
        else:
            k = pool[0][pool[1] % len(pool[0])]
            pool[1] += 1
        prev = self.dcnt[k]
        if prev and q.waited.get(k, 0) < prev:
            q.be.wait_ge(self.semh[k], prev)
            q.waited[k] = prev
        self.dcnt[k] = prev + 16
        if nc_ok:
            with self.nc.allow_non_contiguous_dma(reason="small strided table"):
                q.be.dma_start(out=out, in_=in_).then_inc(self.semh[k], 16)
        else:
            q.be.dma_start(out=out, in_=in_).then_inc(self.semh[k], 16)
        self._commit(k, prev + 16, reads, writes, adds)

    def barrier(self, skip=(), skip_sems=()):
        deps = {}
        skipk = set(skip_sems)
        for e in skip:
            skipk.update(e.pool)
        for e in self.engs:
            if e.cnt and e not in skip:
                deps[e.key] = e.cnt
        for k, v in self.dcnt.items():
            if v and k not in skipk:
                deps[k] = v
        for e in self.engs:
            if e not in skip:
                self._wait(e, deps)


class T:
    def __init__(self, t, name):
        self.t = t
        self.b = Buf(name)

    def __getitem__(self, k):
        return self.t[k]


def build(cfg):
    nc = bass.Bass("TRN2", target_bir_lowering=False)
    NE, NTC, NMIX = cfg.NE, cfg.NTC, cfg.NMIX
    dbg = cfg.debug

    def din(name, shape, dt=F32):
        return nc.dram_tensor(name, list(shape), dt, kind="ExternalInput").ap()

    def dscr(name, shape, dt, out=False):
        kind = "ExternalOutput" if (out and dbg) else "Internal"
        return nc.dram_tensor(name, list(shape), dt, kind=kind).ap()

    xe = din("xe", [NE, D])
    w_in = din("w_in", [D, INW])
    w_pool = din("w_pool", [POOLW, 256])
    w_po = din("w_po", [POOLW, D])
    w_ao = din("w_ao", [512, D])
    w_o = din("w_o", [D, D])
    w_up = din("w_up", [D, 2 * DFF])
    w_down = din("w_down", [DFF, D])
    g1 = din("g1", [1, D])
    g2 = din("g2", [1, D])
    gq = din("gq", [1, HD])
    gk = din("gk", [1, HD])
    b_gate = din("b_gate", [2 * D])
    pscale = din("pscale", [POOLW])
    conv_w = din("conv_w", [3, DFF])
    conv_b = din("conv_b", [DFF])
    ident_d = din("ident", [128, 128], BF16)
    ones_d = din("ones", [128, 128], BF16)
    mask_d = din("mask12", [128, 256])
    negm_d = din("negm", [128, 1024], BF16)
    ropeC = din("ropeC", [NE, HD])
    ropeS = din("ropeS", [NE, HD])
    tokv = din("tokvalid", [1, NE])
    invc = din("invcnt", [4, NE])
    nkt = max(len(attn_units(n)) for _, n in cfg.mix_tiles) * 2
    kvtab = din("kvtab", [len(cfg.mix_tiles), 128, nkt])

    y = nc.dram_tensor("y", [NTC, D], F32, kind="ExternalOutput").ap()

    wb_in = dscr("wb_in", [D, INW], BF16)
    wb_pool = dscr("wb_pool", [POOLW, 256], BF16)
    wb_po = dscr("wb_po", [POOLW, D], BF16)
    wb_ao = dscr("wb_ao", [512, D], BF16)
    wb_o = dscr("wb_o", [D, D], BF16)
    wb_up = dscr("wb_up", [D, 2 * DFF], BF16)
    wb_down = dscr("wb_down", [DFF, D], BF16)
    KT = dscr("KT", [12, HD, NE], BF16, out=True)
    V = dscr("V", [NE, AW], BF16, out=True)
    X1 = dscr("X1", [NMIX, D], F32, out=True)

    es = ExitStack()
    with es:
        S = Sched(nc, es)
        PE, ACT, DVE, POOL, SP = S.pe, S.act, S.dve, S.pool, S.sp
        castpool = [POOL.pool[-3:], 0]
        POOL.pool = POOL.pool[:-3]

        def sb(stack, name, shape, dt):
            return T(stack.enter_context(nc.sbuf_tensor("sb_" + name, list(shape), dt)), name)

        ident = sb(es, "ident", [128, 128], BF16)
        ones = sb(es, "ones", [128, 128], BF16)
        mask12 = sb(es, "mask12", [128, 256], F32)
        negm = sb(es, "negm", [128, 2, 4, 128], BF16)
        gq4 = sb(es, "gq4", [128, 512], F32)
        gk4 = sb(es, "gk4", [128, 512], F32)
        bgh = sb(es, "bgh", [128, 32], F32)
        psh = sb(es, "psh", [128, 8], F32)
        cw = sb(es, "cw", [128, 3, FC], F32)
        cbv = sb(es, "cbv", [128, FC], F32)
        cst = sb(es, "cst", [128, 16], F32)
        es_g1 = ExitStack()
        gb1 = sb(es_g1, "gb1", [128, D], F32)
        NW = 3
        wring = []
        wstate = {"i": 0}
        psum = [T(es.enter_context(nc.psum_tensor(f"ps{i}", [128, 512], F32)), f"ps{i}") for i in range(8)]

        def psbf(i):
            return psum[i].t[:, :].bitcast(BF16).rearrange("p (c t) -> p c t", c=8)

        S.dma(POOL, ident[:, :], ident_d[:, :], writes=[ident.b])
        S.dma(POOL, ones[:, :], ones_d[:, :], writes=[ones.b])
        S.dma(POOL, mask12[:, :], mask_d[:, :], writes=[mask12.b])
        S.dma(POOL, negm[:, :, :, :], negm_d.rearrange("p (t h i) -> p t h i", t=2, h=4), writes=[negm.b])
        S.dma(POOL, gb1[:, :], g1[0:1, :].partition_broadcast(128), writes=[gb1.b])
        for h in range(4):
            S.dma(POOL, gq4[:, h * 128:(h + 1) * 128], gq[0:1, :].partition_broadcast(128), adds=[gq4.b])
            S.dma(POOL, gk4[:, h * 128:(h + 1) * 128], gk[0:1, :].partition_broadcast(128), adds=[gk4.b])
        S.dma(POOL, bgh[:, :], b_gate.rearrange("(c p) -> p c", p=128), writes=[bgh.b], nc_ok=True)
        S.dma(POOL, psh[:, :], pscale.rearrange("(c p) -> p c", p=128), writes=[psh.b], nc_ok=True)
        for t3 in range(3):
            S.dma(POOL, cw[:, t3, :], conv_w[t3, :].rearrange("(c p) -> p c", p=128), adds=[cw.b], nc_ok=True)
        S.dma(POOL, cbv[:, :], conv_b.rearrange("(c p) -> p c", p=128), writes=[cbv.b], nc_ok=True)
        S.op(DVE, lambda e: e.memset(cst[:, 0:8], EPS), writes=[cst.b])
        S.op(ACT, lambda e: e.mul(bgh[:, :], bgh[:, :], 0.5), writes=[bgh.b])
        S.op(ACT, lambda e: e.mul(psh[:, :], psh[:, :], 0.5), writes=[psh.b])

        wbuf = {n: Buf(n) for n in ("in_kv", "in_rest", "pool", "po", "ao", "o", "up", "down")}
        casts = []

        def cast_rows(dst, src, bufname, c0=None, c1=None, rows=128, cols=2048):
            nrows = src.shape[0]
            if c0 is None:
                c0, c1 = 0, src.shape[1]
            for r0 in range(0, nrows, rows):
                r1 = min(nrows, r0 + rows)
                for cc in range(c0, c1, cols):
                    ce = min(c1, cc + cols)
                    casts.append((dst[r0:r1, cc:ce], src[r0:r1, cc:ce], bufname))

        n_first = 0
        cast_rows(wb_in, w_in, "in_rest", 0, KOFF)
        cast_rows(wb_in, w_in, "in_rest", GOFF, INW)
        cast_rows(wb_pool, w_pool, "pool")
        cast_rows(wb_po, w_po, "po")
        cast_rows(wb_ao, w_ao, "ao")
        cast_rows(wb_o, w_o, "o")
        cast_rows(wb_up, w_up, "up")
        cast_rows(wb_down, w_down, "down")
        cast_pos = {"i": 0}

        def emit_casts(n):
            while n > 0 and cast_pos["i"] < len(casts):
                dst, src, bn = casts[cast_pos["i"]]
                cast_pos["i"] += 1
                S.dma(POOL, dst, src, adds=[wbuf[bn]], pool=castpool)
                n -= 1

        wkv_es = ExitStack()
        if 1 in cfg.sweeps:
            wkv = sb(wkv_es, "wkv", [128, KC, 6 * 512], BF16)
            wkvb = [Buf(f"wkv{cb}") for cb in range(6)]
            for cb in (0, 3, 1, 4, 2, 5):
                S.dma(POOL, wkv[:, :, cb * 512:(cb + 1) * 512],
                      w_in[:, KOFF + cb * 512:KOFF + (cb + 1) * 512].rearrange("(k p) c -> p k c", p=128),
                      writes=[wkvb[cb]])
        if 0 in cfg.sweeps:
            emit_casts(len(casts))

        def wload(wb, wname, k0, nk, c0, ncols=512):
            slot = wring[wstate["i"] % NW]
            wstate["i"] += 1
            src = wb[k0 * 128:(k0 + nk) * 128, c0:c0 + ncols].rearrange("(k p) c -> p k c", p=128)
            S.dma(SP, slot[:, 0:nk, 0:ncols], src, reads=[wbuf[wname]], writes=[slot.b])
            return slot

        def wload_into(slot, kofs, wb, wname, k0, nk, c0, ncols=512):
            src = wb[k0 * 128:(k0 + nk) * 128, c0:c0 + ncols].rearrange("(k p) c -> p k c", p=128)
            S.dma(SP, slot[:, kofs:kofs + nk, 0:ncols], src, reads=[wbuf[wname]], adds=[slot.b])

        LQ = {"q": SP}
        kvb = Buf("kvscratch")
        x1b = [Buf(f"x1_{i}") for i in range(len(cfg.mix_tiles))]

        def x1_deps(m_lo, n):
            return [x1b[i] for i, (a_, nt_) in enumerate(cfg.mix_tiles)
                    if a_ - 1024 < m_lo + n and a_ - 1024 + nt_ > m_lo]
        uid = {"n": 0}

        def un(name):
            uid["n"] += 1
            return f"{name}_{uid['n']}"

        def cbarrier():
            S.barrier(skip=(SP,))

        def norm_stages(stk_bufs, src_fn, n, gbt, hT, hTb, col0, src_reads=()):
            xt, hb, ssb = stk_bufs
            out = []
            for si, (off, nt) in enumerate(subtiles(n)):
                s = si % 2

                def n1(si=si, off=off, nt=nt, s=s):
                    S.dma(LQ['q'], xt[s][:nt, :], src_fn(off, nt), reads=list(src_reads), writes=[xt[s].b])
                    S.op(ACT, lambda e: e.activation(out=hb[s][:nt, :], in_=xt[s][:nt, :], func=AF.Square,
                                                     accum_out=ssb[s][:nt, 0:1]),
                         reads=[xt[s].b], writes=[hb[s].b, ssb[s].b])
                    S.op(ACT, lambda e: e.activation(out=ssb[s][:nt, 1:2], in_=ssb[s][:nt, 0:1], func=AF.Sqrt,
                                                     bias=cst[:nt, 0:1], scale=1.0 / D),
                         reads=[cst.b], writes=[ssb[s].b])
                    S.op(DVE, lambda e: e.reciprocal(out=ssb[s][:nt, 1:2], in_=ssb[s][:nt, 1:2]), writes=[ssb[s].b])
                    S.op(DVE, lambda e: e.scalar_tensor_tensor(out=hb[s][:nt, :], in0=xt[s][:nt, :],
                                                               scalar=ssb[s][:nt, 1:2], in1=gbt[:nt, :],
                                                               op0=ALU.mult, op1=ALU.mult),
                         reads=[xt[s].b, ssb[s].b, gbt.b], writes=[hb[s].b])

                def n2(si=si, off=off, nt=nt, s=s):
                    for half in range(2):
                        pb = psbf(half)
                        for c8 in range(8):
                            c = half * 8 + c8
                            S.op(PE, lambda e: e.transpose(pb[:, c8, :nt], hb[s][:nt, c * 128:(c + 1) * 128],
                                                           ident[:nt, :nt]),
                                 reads=[hb[s].b, ident.b], writes=[psum[half].b], inc=(c8 == 7))
                        dst = hT[:, half * 8:(half + 1) * 8, col0 + off:col0 + off + nt]
                        if half == 0:
                            S.op(ACT, lambda e: e.activation(out=dst, in_=pb[:, :, :nt], func=AF.Copy),
                                 reads=[psum[half].b], adds=[hTb[si]])
                        else:
                            S.op(DVE, lambda e: e.tensor_copy(out=dst, in_=pb[:, :, :nt]),
                                 reads=[psum[half].b], adds=[hTb[si]])
                out.append((n1, n2))
            return out

        def norm_T(stk_bufs, src_fn, n, gbt, hT, hTb, col0, src_reads=()):
            for n1, n2 in norm_stages(stk_bufs, src_fn, n, gbt, hT, hTb, col0, src_reads):
                n1()
                n2()

        def qk_norm_rope(bufs, src_ps, nt, g4, rc, rs_, outb, beng=None):
            junk, ss4, kn, A, B = bufs
            beng = beng or DVE
            for h in range(4):
                S.op(ACT, lambda e: e.activation(out=junk[:nt, h * 128:(h + 1) * 128],
                                                 in_=src_ps.t[:nt, h * 128:(h + 1) * 128], func=AF.Square,
                                                 accum_out=ss4[:nt, h:h + 1]),
                     reads=[src_ps.b], adds=[junk.b, ss4.b] if h else [junk.b], writes=[] if h else [ss4.b])
            S.op(ACT, lambda e: e.activation(out=ss4[:nt, 4:8], in_=ss4[:nt, 0:4], func=AF.Sqrt,
                                             bias=cst[:nt, 0:1], scale=1.0 / HD), reads=[cst.b], writes=[ss4.b])
            S.op(DVE, lambda e: e.reciprocal(out=ss4[:nt, 4:8], in_=ss4[:nt, 4:8]), writes=[ss4.b])
            for h in range(4):
                S.op(DVE, lambda e: e.scalar_tensor_tensor(out=kn[:nt, h * 128:(h + 1) * 128],
                                                           in0=src_ps.t[:nt, h * 128:(h + 1) * 128],
                                                           scalar=ss4[:nt, 4 + h:5 + h], in1=g4[:nt, h * 128:(h + 1) * 128],
                                                           op0=ALU.mult, op1=ALU.mult),
                     reads=[src_ps.b, ss4.b, g4.b], adds=[kn.b] if h else [], writes=[] if h else [kn.b])
            S.op(DVE, lambda e: e.tensor_tensor(out=A[:nt, :].rearrange("p (h d) -> p h d", h=4),
                                                in0=kn[:nt, :].rearrange("p (h d) -> p h d", h=4),
                                                in1=rc[:nt, :].unsqueeze(1).to_broadcast([nt, 4, HD]), op=ALU.mult),
                 reads=[kn.b, rc.b], writes=[A.b])
            knv = kn[:nt, :].rearrange("p (h t d) -> p h t d", h=4, t=2)
            Bv = B[:nt, :].rearrange("p (h t d) -> p h t d", h=4, t=2)
            S.op(beng, lambda e: e.tensor_tensor(out=Bv[:, :, 0, :], in0=knv[:, :, 1, :],
                                                 in1=rs_[:nt, 0:64].unsqueeze(1).to_broadcast([nt, 4, 64]),
                                                 op=ALU.mult), reads=[kn.b, rs_.b], writes=[B.b])
            S.op(beng, lambda e: e.tensor_tensor(out=Bv[:, :, 1, :], in0=knv[:, :, 0, :],
                                                 in1=rs_[:nt, 64:128].unsqueeze(1).to_broadcast([nt, 4, 64]),
                                                 op=ALU.mult), reads=[kn.b, rs_.b], adds=[B.b])
            S.op(DVE, lambda e: e.tensor_tensor(out=outb[:nt, :], in0=A[:nt, :], in1=B[:nt, :], op=ALU.add),
                 reads=[A.b, B.b], writes=[outb.b])

        def load_rope(rc, rs_, u0, nt):
            S.dma(LQ['q'], rc[:nt, :], ropeC[u0:u0 + nt, :], writes=[rc.b])
            S.dma(LQ['q'], rs_[:nt, :], ropeS[u0:u0 + nt, :], writes=[rs_.b])

        if 1 in cfg.sweeps:
            with ExitStack() as s1:
                xt = [sb(s1, f"xt{i}", [128, D], F32) for i in range(2)]
                hb = [sb(s1, f"hb{i}", [128, D], BF16) for i in range(2)]
                ssb = [sb(s1, f"ssb{i}", [128, 2], F32) for i in range(2)]
                hT2 = [sb(s1, f"hT{i}", [128, KC, 512], BF16) for i in range(2)]
                hTb2 = [[Buf(f"hT{k}_{i}") for i in range(4)] for k in range(2)]
                qkbufs = [(sb(s1, f"junk{i}", [128, 512], BF16), sb(s1, f"ss4{i}", [128, 8], F32),
                           sb(s1, f"kn{i}", [128, 512], F32), sb(s1, f"A{i}", [128, 512], F32),
                           sb(s1, f"B{i}", [128, 512], F32)) for i in range(2)]
                kcnt = 0
                rc = [sb(s1, f"rc{i}", [128, HD], F32) for i in range(8)]
                rs_ = [sb(s1, f"rs{i}", [128, HD], F32) for i in range(8)]
                kb = [sb(s1, f"kb{i}", [128, 512], BF16) for i in range(3)]
                vb = [sb(s1, f"vb{i}", [128, 512], BF16) for i in range(2)]
                kst = [sb(s1, f"kst{i}", [128, 4, 512], BF16) for i in range(2)]
                cnt = 0
                def s1_norm(ti):
                    u0_, ntok_ = cfg.s1_tiles[ti]
                    return norm_stages((xt, hb, ssb), lambda off, nt, u0_=u0_: xe[u0_ + off:u0_ + off + nt, :], ntok_,
                                       gb1, hT2[ti % 2], hTb2[ti % 2], 0)

                for n1, n2 in s1_norm(0):
                    n1()
                    n2()
                for ti, (u0, ntok) in enumerate(cfg.s1_tiles):
                    sub = subtiles(ntok)
                    hT, hTb = hT2[ti % 2], hTb2[ti % 2]
                    nsched = {}
                    if ti + 1 < len(cfg.s1_tiles):
                        for k, (n1, n2) in enumerate(s1_norm(ti + 1)):
                            nsched.setdefault(4 + 3 * k, []).append(n1)
                            nsched.setdefault(6 + 3 * k, []).append(n2)
                    rcs, rss = rc[(ti % 2) * 4:(ti % 2) * 4 + 4], rs_[(ti % 2) * 4:(ti % 2) * 4 + 4]
                    for si, (off, nt) in enumerate(sub):
                        load_rope(rcs[si], rss[si], u0 + off, nt)
                    pend = []
                    ucount = 0
                    for cb in (0, 3, 1, 4, 2, 5):
                        lo_, hi_ = 1024 - 64 * DIL[cb % 3], 1024 + NTC + 2 + 64 * DIL[cb % 3]
                        act = [(si, off, nt) for si, (off, nt) in enumerate(sub)
                               if u0 + off < hi_ and u0 + off + nt > lo_]
                        for ai, (si, off, nt) in enumerate(act):
                            first_, last_ = (ai == 0), (ai == len(act) - 1)
                            c_lo, c_hi = act[0][1], act[-1][1] + act[-1][2]
                            bank = psum[2 + (cnt % 4)]
                            cnt += 1
                            for kc in range(KC):
                                S.op(PE, lambda e: e.matmul(bank.t[:nt, :], lhsT=hT[:, kc, off:off + nt],
                                                            rhs=wkv[:, kc, cb * 512:(cb + 1) * 512], start=(kc == 0),
                                                            stop=(kc == KC - 1)),
                                     reads=[hTb[si], wkvb[cb]], writes=[bank.b], inc=(kc == KC - 1))
                            if cb < 3:
                                kbs = kb[kcnt % 3]
                                qk_norm_rope(qkbufs[kcnt % 2], bank, nt, gk4, rcs[si], rss[si], kbs)
                                pbi = 6 + kcnt % 2
                                kcnt += 1

                                def fin(kbs=kbs, pbi=pbi, cb=cb, si=si, off=off, nt=nt, last=last_, first=first_,
                                        u0=u0, c_lo=c_lo, c_hi=c_hi):
                                    pb = psbf(pbi)
                                    for h in range(4):
                                        S.op(PE, lambda e: e.transpose(pb[:, h, :nt], kbs[:nt, h * 128:(h + 1) * 128],
                                                                       ident[:nt, :nt]),
                                             reads=[kbs.b, ident.b], writes=[psum[pbi].b], inc=(h == 3))
                                    ks = kst[cb % 2]
                                    S.op(ACT, lambda e: e.activation(out=ks[:, :, off:off + nt], in_=pb[:, 0:4, :nt],
                                                                     func=AF.Copy),
                                         reads=[psum[pbi].b], writes=[ks.b] if first else [],
                                         adds=[] if first else [ks.b])
                                    if last:
                                        S.dma(ACT, KT[cb * 4:(cb + 1) * 4, :, u0 + c_lo:u0 + c_hi].rearrange("h d t -> d h t"),
                                              ks[:, :, c_lo:c_hi], reads=[ks.b], adds=[kvb])
                                pend.append(fin)
                            else:
                                vbs = vb[cnt % 2]
                                S.op(ACT, lambda e: e.activation(out=vbs[:nt, :], in_=bank.t[:nt, :], func=AF.Copy),
                                     reads=[bank.b], writes=[vbs.b])
                                S.dma(ACT, V[u0 + off:u0 + off + nt, (cb - 3) * 512:(cb - 2) * 512], vbs[:nt, :],
                                      reads=[vbs.b], adds=[kvb])
                            while len(pend) > 2:
                                pend.pop(0)()
                            for fn_ in nsched.pop(ucount, []):
                                fn_()
                            ucount += 1
                    while pend:
                        pend.pop(0)()
                    for k_ in sorted(nsched):
                        for fn_ in nsched[k_]:
                            fn_()
                S.barrier(skip_sems=castpool[0])
        if 0 in cfg.sweeps:
            emit_casts(len(casts))


        wkv_es.close()
        es_r2 = ExitStack()
        for i in range(NW):
            wring.append(sb(es_r2, f"wr{i}", [128, 16, 512], BF16))

        SCALE = 1.0 / math.sqrt(HD)
        if 2 in cfg.sweeps:
            LQ["q"] = POOL
            with ExitStack() as s2:
                hTs = [sb(s2, f"hT2_{k}", [128, KC, 528], BF16) for k in range(2)]
                hTbs = [[Buf(f"hT2_{k}_{i}") for i in range(5)] for k in range(2)]

                def s2_norm(jj, bufs3):
                    a_, ntok_ = cfg.mix_tiles[jj]
                    return norm_stages(bufs3, lambda off, nt, a_=a_: xe[a_ - 8 + off:a_ - 8 + off + nt, :], ntok_ + 16,
                                       gb1, hTs[jj % 2], hTbs[jj % 2], 0)

                with ExitStack() as p0:
                    xt0 = [sb(p0, un("xt"), [128, D], F32) for i in range(2)]
                    hb0 = [sb(p0, un("hb"), [128, D], BF16) for i in range(2)]
                    ssb0 = [sb(p0, un("ssb"), [128, 2], F32) for i in range(2)]
                    for n1, n2 in s2_norm(0, (xt0, hb0, ssb0)):
                        n1()
                        n2()
                    cbarrier()
                attnT = sb(s2, "attnT", [128, 4, 512], BF16)
                pmT = sb(s2, "pmT", [128, 8, 512], BF16)
                wpl = sb(s2, "wpl", [128, 4, 2, 256], BF16)
                for g in range(4):
                    S.dma(POOL, wpl[:, g, :, :], wb_pool[g * 256:(g + 1) * 256, :].rearrange("(k p) c -> p k c", p=128),
                          reads=[wbuf["pool"]], adds=[wpl.b])
                for j, (a, ntok) in enumerate(cfg.mix_tiles):
                    sub = subtiles(ntok)
                    nh = ntok + 16
                    hT, hTb = hTs[j % 2], hTbs[j % 2]
                    with ExitStack() as pab:
                        qT = sb(pab, un("qT"), [128, 12, 512], BF16)
                        kw = [sb(pab, un("kw"), [128, 4, 512 + 128 * Dg], BF16) for Dg in DIL]
                        kvt = sb(pab, un("kvt"), [128, nkt], F32)
                        for g, Dg in enumerate(DIL):
                            S.dma(POOL, kw[g][:, :, 0:ntok + 128 * Dg],
                                  KT[4 * g:4 * g + 4, :, a - 64 * Dg:a + ntok + 64 * Dg].rearrange("h d t -> d h t"),
                                  reads=[kvb], writes=[kw[g].b])
                        S.dma(POOL, kvt[:, :], kvtab[j, :, :], writes=[kvt.b])
                        with ExitStack() as pa:
                            qkbufs = [(sb(pa, un("junk"), [128, 512], BF16), sb(pa, un("ss4"), [128, 8], F32),
                                       sb(pa, un("kn"), [128, 512], F32), sb(pa, un("A"), [128, 512], F32),
                                       sb(pa, un("B"), [128, 512], F32)) for i in range(2)]
                            rc = [sb(pa, un("rc"), [128, HD], F32) for i in range(4)]
                            rs_ = [sb(pa, un("rs"), [128, HD], F32) for i in range(4)]
                            qb = [sb(pa, un("qb"), [128, 512], BF16) for i in range(3)]
                            for si, (off, nt) in enumerate(sub):
                                load_rope(rc[si], rs_[si], a + off, nt)
                            cnt = 0
                            pend = []
                            for cb in range(3):
                                w = wload(wb_in, "in_rest", 0, KC, QOFF + cb * 512)
                                for si, (off, nt) in enumerate(sub):
                                    bank = psum[2 + (cnt % 4)]
                                    for kc in range(KC):
                                        S.op(PE, lambda e: e.matmul(bank.t[:nt, :], lhsT=hT[:, kc, 8 + off:8 + off + nt],
                                                                    rhs=w[:, kc, :], start=(kc == 0), stop=(kc == KC - 1)),
                                             reads=hTb + [w.b], writes=[bank.b], inc=(kc == KC - 1))
                                    qbs = qb[cnt % 3]
                                    qk_norm_rope(qkbufs[cnt % 2], bank, nt, gq4, rc[si], rs_[si], qbs, beng=POOL)
                                    pbi = 6 + cnt % 2
                                    cnt += 1

                                    def fin(qbs=qbs, pbi=pbi, cb=cb, off=off, nt=nt):
                                        pb = psbf(pbi)
                                        for h in range(4):
                                            S.op(PE, lambda e: e.transpose(pb[:, h, :nt], qbs[:nt, h * 128:(h + 1) * 128],
                                                                           ident[:nt, :nt]),
                                                 reads=[qbs.b, ident.b], writes=[psum[pbi].b], inc=(h == 3))
                                        S.op(ACT, lambda e: e.activation(out=qT[:, cb * 4:(cb + 1) * 4, off:off + nt],
                                                                         in_=pb[:, 0:4, :nt], func=AF.Copy),
                                             reads=[psum[pbi].b], adds=[qT.b])
                                    pend.append(fin)
                                    while len(pend) > 2:
                                        pend.pop(0)()
                            while pend:
                                pend.pop(0)()
                        cbarrier()
                        with ExitStack() as pbk:
                            NSB = 3
                            p1 = [sb(pbk, un("p1"), [128, 512], BF16) for i in range(NSB)]
                            p2 = [sb(pbk, un("p2"), [128, 512], BF16) for i in range(NSB)]
                            vt1 = [sb(pbk, un("vt1"), [128, 512], BF16) for i in range(NSB)]
                            vt2 = [sb(pbk, un("vt2"), [128, 512], BF16) for i in range(NSB)]
                            accO = sb(pbk, un("accO"), [128, 4, 512], F32)
                            accD = sb(pbk, un("accD"), [128, 4, 512], F32)
                            units = attn_units(ntok)

                            def sta(ui):
                                g, r, l0, nq = units[ui]
                                Dg = DIL[g]
                                c1 = r + Dg * l0
                                c2 = c1 + 128 * Dg
                                u1 = a - 64 * Dg + c1
                                u2 = a - 64 * Dg + c2
                                v1, v2 = vt1[ui % NSB], vt2[ui % NSB]
                                S.dma(POOL, v1[:, :], V[u1:u1 + Dg * 127 + 1:Dg, g * 512:(g + 1) * 512],
                                      reads=[kvb], writes=[v1.b])
                                S.dma(POOL, v2[:nq, :], V[u2:u2 + Dg * (nq - 1) + 1:Dg, g * 512:(g + 1) * 512],
                                      reads=[kvb], writes=[v2.b])
                                qs = slice(c1, c1 + Dg * (nq - 1) + 1, Dg)
                                bS1, bS2 = psum[2 * (ui % 2)], psum[2 * (ui % 2) + 1]
                                P1, P2 = p1[ui % NSB], p2[ui % NSB]
                                bS1v = bS1.t[:, 0:4 * nq].rearrange("p (h i) -> p h i", h=4)
                                bS2v = bS2.t[:nq, 0:4 * nq].rearrange("p (h i) -> p h i", h=4)
                                S.op(PE, lambda e: e.matmul(bS1v, lhsT=ident[:, :], rhs=negm[:, 0, :, 0:nq],
                                                            start=True, stop=False),
                                     reads=[ident.b, negm.b], writes=[bS1.b], inc=False)
                                for h in range(4):
                                    S.op(PE, lambda e: e.matmul(bS1.t[:, h * nq:(h + 1) * nq],
                                                                lhsT=kw[g][:, h, c1:c1 + Dg * 127 + 1:Dg],
                                                                rhs=qT[:, 4 * g + h, qs], start=False, stop=(h == 3)),
                                         reads=[kw[g].b, qT.b], adds=[bS1.b], inc=(h == 3))
                                S.op(PE, lambda e: e.matmul(bS2v, lhsT=ident[:nq, :nq], rhs=negm[:nq, 1, :, 0:nq],
                                                            start=True, stop=False),
                                     reads=[ident.b, negm.b], writes=[bS2.b], inc=False)
                                for h in range(4):
                                    S.op(PE, lambda e: e.matmul(bS2.t[:nq, h * nq:(h + 1) * nq],
                                                                lhsT=kw[g][:, h, c2:c2 + Dg * (nq - 1) + 1:Dg],
                                                                rhs=qT[:, 4 * g + h, qs], start=False, stop=(h == 3)),
                                         reads=[kw[g].b, qT.b], adds=[bS2.b], inc=(h == 3))
                                S.op(ACT, lambda e: e.activation(out=P1[:, :4 * nq], in_=bS1.t[:, 0:4 * nq], func=AF.Exp,
                                                                 bias=kvt[:, 2 * ui:2 * ui + 1], scale=SCALE),
                                     reads=[bS1.b, kvt.b], writes=[P1.b])
                                S.op(ACT, lambda e: e.activation(out=P2[:nq, :4 * nq], in_=bS2.t[:nq, 0:4 * nq], func=AF.Exp,
                                                                 bias=kvt[:nq, 2 * ui + 1:2 * ui + 2], scale=SCALE),
                                     reads=[bS2.b, kvt.b], writes=[P2.b])

                            def stb(ui):
                                g, r, l0, nq = units[ui]
                                Dg = DIL[g]
                                c1 = r + Dg * l0
                                qs = slice(c1, c1 + Dg * (nq - 1) + 1, Dg)
                                v1, v2 = vt1[ui % NSB], vt2[ui % NSB]
                                P1, P2 = p1[ui % NSB], p2[ui % NSB]
                                bO, bD = psum[4 + 2 * (ui % 2)], psum[5 + 2 * (ui % 2)]
                                for h in range(4):
                                    S.op(PE, lambda e: e.matmul(bO.t[:, h * nq:(h + 1) * nq], lhsT=v1[:, h * 128:(h + 1) * 128],
                                                                rhs=P1[:, h * nq:(h + 1) * nq], start=True, stop=False),
                                         reads=[v1.b, P1.b], writes=[bO.b] if h == 0 else [],
                                         adds=[] if h == 0 else [bO.b], inc=False)
                                    S.op(PE, lambda e: e.matmul(bO.t[:, h * nq:(h + 1) * nq], lhsT=v2[:nq, h * 128:(h + 1) * 128],
                                                                rhs=P2[:nq, h * nq:(h + 1) * nq], start=False, stop=True),
                                         reads=[v2.b, P2.b], adds=[bO.b], inc=(h == 3))
                                S.op(PE, lambda e: e.matmul(bD.t[:, 0:4 * nq], lhsT=ones[:, :], rhs=P1[:, 0:4 * nq],
                                                            start=True, stop=False),
                                     reads=[ones.b, P1.b], writes=[bD.b], inc=False)
                                S.op(PE, lambda e: e.matmul(bD.t[:, 0:4 * nq], lhsT=ones[:nq, :], rhs=P2[:nq, 0:4 * nq],
                                                            start=False, stop=True),
                                     reads=[ones.b, P2.b], adds=[bD.b])
                                bOv = bO.t[:, 0:4 * nq].rearrange("p (h i) -> p h i", h=4)
                                bDv = bD.t[:, 0:4 * nq].rearrange("p (h i) -> p h i", h=4)
                                if g == 0:
                                    S.op(ACT, lambda e: e.activation(out=accO[:, :, qs], in_=bOv, func=AF.Copy),
                                         reads=[bO.b], adds=[accO.b])
                                    S.op(DVE, lambda e: e.tensor_copy(out=accD[:, :, qs], in_=bDv),
                                         reads=[bD.b], adds=[accD.b])
                                else:
                                    S.op(DVE, lambda e: e.tensor_tensor(out=accO[:, :, qs], in0=accO[:, :, qs], in1=bOv,
                                                                        op=ALU.add), reads=[bO.b], writes=[accO.b])
                                    S.op(DVE, lambda e: e.tensor_tensor(out=accD[:, :, qs], in0=accD[:, :, qs], in1=bDv,
                                                                        op=ALU.add), reads=[bD.b], writes=[accD.b])

                            for t in range(len(units) + 1):
                                if t < len(units):
                                    sta(t)
                                if t >= 1:
                                    stb(t - 1)
                            S.op(DVE, lambda e: e.reciprocal(out=accD[:, :, :ntok], in_=accD[:, :, :ntok]),
                                 writes=[accD.b])
                            S.op(DVE, lambda e: e.scalar_tensor_tensor(out=attnT[:, :, :ntok], in0=accO[:, :, :ntok],
                                                                       scalar=0.5, in1=accD[:, :, :ntok],
                                                                       op0=ALU.mult, op1=ALU.mult),
                                 reads=[accO.b, accD.b], writes=[attnT.b])
                        cbarrier()
                    with ExitStack() as pc:
                        up = sb(pc, un("up"), [128, 8, 528], F32)
                        upb = [Buf(f"up{g}") for g in range(4)]
                        tbs = [[sb(pc, un("Pa"), [128, 2, 528], F32), sb(pc, un("Pb"), [128, 2, 528], F32)]
                               for i in range(2)]
                        ic = sb(pc, un("ic"), [128, 4, 512], F32)
                        diffT = sb(pc, un("diffT"), [128, 8, 512], BF16)
                        dfb = [Buf(f"df{g}") for g in range(4)]
                        for g in range(4):
                            S.dma(POOL, ic[:, g, :ntok], invc[g:g + 1, a:a + ntok].partition_broadcast(128), adds=[ic.b])
                        splits = [(n0, min(512, nh - n0)) for n0 in range(0, nh, 512)]
                        cnt = 0

                        def pool_group(g, eng, tb):
                            R = POOL_R[g]
                            uv = up[:, 2 * g:2 * g + 2, :]
                            cur, ln, m = uv, nh, 1
                            curb = upb[g]
                            ti = 0
                            while m < 2 * R:
                                nxt = tb[ti]
                                ti ^= 1
                                S.op(eng, lambda e: e.tensor_tensor(out=nxt[:, :, 0:ln - m], in0=cur[:, :, 0:ln - m],
                                                                    in1=cur[:, :, m:ln], op=ALU.add),
                                     reads=[curb], writes=[nxt.b])
                                cur, curb = nxt.t, nxt.b
                                ln -= m
                                m *= 2
                            Wt = tb[ti]
                            S.op(eng, lambda e: e.tensor_tensor(out=Wt[:, :, 0:ntok], in0=cur[:, :, 8 - R:8 - R + ntok],
                                                                in1=uv[:, :, 8 + R:8 + R + ntok], op=ALU.add),
                                 reads=[curb, upb[g]], writes=[Wt.b])
                            Tt = tb[ti ^ 1]
                            S.op(eng, lambda e: e.tensor_tensor(out=Tt[:, :, 0:ntok], in0=Wt[:, :, 0:ntok],
                                                                in1=ic[:, g:g + 1, :ntok].to_broadcast([128, 2, ntok]),
                                                                op=ALU.mult),
                                 reads=[Wt.b, ic.b], writes=[Tt.b])
                            S.op(eng, lambda e: e.tensor_tensor(out=diffT[:, 2 * g:2 * g + 2, :ntok], in0=Tt[:, :, 0:ntok],
                                                                in1=uv[:, :, 8:8 + ntok], op=ALU.subtract),
                                 reads=[Tt.b, upb[g]], writes=[dfb[g]])

                        def pool_mm(g):
                            nonlocal_cnt = [0]
                            for oc in range(2):
                                bank = psum[4 + (2 * g + oc) % 4]
                                for kc in range(2):
                                    S.op(PE, lambda e: e.matmul(bank.t[:, 0:ntok], lhsT=wpl[:, g, kc, oc * 128:(oc + 1) * 128],
                                                                rhs=diffT[:, 2 * g + kc, :ntok], start=(kc == 0),
                                                                stop=(kc == 1)),
                                         reads=[wpl.b, dfb[g]], writes=[bank.b], inc=(kc == 1))
                                S.op(ACT, lambda e: e.activation(out=pmT[:, 2 * g + oc, :ntok], in_=bank.t[:, 0:ntok],
                                                                 func=AF.Copy, scale=psh[:, 2 * g + oc:2 * g + oc + 1]),
                                     reads=[bank.b, psh.b], adds=[pmT.b])

                        for blk in range(2):
                            w = wload(wb_in, "in_rest", 0, KC, blk * 512)
                            for ci in range(4):
                                c = blk * 4 + ci
                                for (n0, nn) in splits:
                                    bank = psum[cnt % 4]
                                    cnt += 1
                                    for kc in range(KC):
                                        S.op(PE, lambda e: e.matmul(bank.t[:, 0:nn], lhsT=w[:, kc, ci * 128:(ci + 1) * 128],
                                                                    rhs=hT[:, kc, n0:n0 + nn], start=(kc == 0),
                                                                    stop=(kc == KC - 1)),
                                             reads=hTb + [w.b], writes=[bank.b], inc=(kc == KC - 1))
                                    S.op(ACT, lambda e: e.activation(out=up[:, c, n0:n0 + nn], in_=bank.t[:, 0:nn],
                                                                     func=AF.Copy), reads=[bank.b], adds=[upb[c // 2]])
                            if blk == 0:
                                pool_group(0, POOL, tbs[0])
                                pool_group(1, DVE, tbs[1])
                            else:
                                pool_mm(0)
                                pool_mm(1)
                                pool_group(2, POOL, tbs[0])
                                pool_group(3, DVE, tbs[1])
                                pool_mm(3)
                                pool_mm(2)
                    cbarrier()
                    pde = ExitStack()
                    mT = sb(pde, un("mT"), [128, KC, 512], BF16)
                    nsched = {}
                    if j + 1 < len(cfg.mix_tiles):
                        xtp = [sb(pde, un("xt"), [128, D], F32) for i in range(2)]
                        hbp = [sb(pde, un("hb"), [128, D], BF16) for i in range(2)]
                        ssbp = [sb(pde, un("ssb"), [128, 2], F32) for i in range(2)]
                        for k, (n1, n2) in enumerate(s2_norm(j + 1, (xtp, hbp, ssbp))):
                            nsched.setdefault(1 + 2 * k, []).append(n1)
                            nsched.setdefault(2 + 2 * k, []).append(n2)
                    dstep = 0
                    xin = [sb(pde, un("xin"), [128, 512], F32) for i in range(2)]
                    xout = [sb(pde, un("xout"), [128, 512], F32) for i in range(2)]
                    with ExitStack() as pd:
                        t0 = [sb(pd, un("t0"), [128, 512], F32) for i in range(4)]
                        t1 = [sb(pd, un("t1"), [128, 512], F32) for i in range(4)]
                        m0 = [sb(pd, un("m0"), [128, 512], F32) for i in range(2)]
                        m1 = [sb(pd, un("m1"), [128, 512], F32) for i in range(2)]
                        cnt = 0
                        for grp in range(4):
                            for br, tt in ((0, t0), (1, t1)):
                                for fn_ in nsched.pop(dstep, []):
                                    fn_()
                                dstep += 1
                                w = wload(wb_in, "in_rest", 0, KC, GOFF + br * D + grp * 512)
                                for ci in range(4):
                                    c = grp * 4 + ci
                                    bank = psum[cnt % 8]
                                    cnt += 1
                                    for kc in range(KC):
                                        S.op(PE, lambda e: e.matmul(bank.t[:, 0:ntok], lhsT=w[:, kc, ci * 128:(ci + 1) * 128],
                                                                    rhs=hT[:, kc, 8:8 + ntok], start=(kc == 0),
                                                                    stop=(kc == KC - 1)),
                                             reads=hTb + [w.b], writes=[bank.b], inc=(kc == KC - 1))
                                    S.op(ACT, lambda e: e.activation(out=tt[ci][:, :ntok], in_=bank.t[:, 0:ntok], func=AF.Tanh,
                                                                     bias=bgh[:, br * 16 + c:br * 16 + c + 1], scale=0.5),
                                         reads=[bank.b, bgh.b], writes=[tt[ci].b])
                            for fn_ in nsched.pop(dstep, []):
                                fn_()
                            dstep += 1
                            wpa = wload(wb_po, "po", 0, 8, grp * 512)
                            wload_into(wpa, 8, wb_ao, "ao", 0, 4, grp * 512)
                            for ci in range(4):
                                c = grp * 4 + ci
                                bP, bA = psum[cnt % 8], psum[(cnt + 1) % 8]
                                cnt += 2
                                for kc in range(8):
                                    S.op(PE, lambda e: e.matmul(bP.t[:, 0:ntok], lhsT=wpa[:, kc, ci * 128:(ci + 1) * 128],
                                                                rhs=pmT[:, kc, :ntok], start=(kc == 0), stop=(kc == 7)),
                                         reads=[wpa.b, pmT.b], writes=[bP.b], inc=(kc == 7))
                                for kc in range(4):
                                    S.op(PE, lambda e: e.matmul(bA.t[:, 0:ntok], lhsT=wpa[:, 8 + kc, ci * 128:(ci + 1) * 128],
                                                                rhs=attnT[:, kc, :ntok], start=(kc == 0), stop=(kc == 3)),
                                         reads=[wpa.b, attnT.b], writes=[bA.b], inc=(kc == 3))
                                M0, M1 = m0[ci % 2], m1[ci % 2]
                                S.op(DVE, lambda e: e.scalar_tensor_tensor(out=M0[:, :ntok], in0=t0[ci][:, :ntok], scalar=1.0,
                                                                           in1=bP.t[:, 0:ntok], op0=ALU.add, op1=ALU.mult),
                                     reads=[t0[ci].b, bP.b], writes=[M0.b])
                                S.op(DVE, lambda e: e.scalar_tensor_tensor(out=M1[:, :ntok], in0=t1[ci][:, :ntok], scalar=1.0,
                                                                           in1=bA.t[:, 0:ntok], op0=ALU.add, op1=ALU.mult),
                                     reads=[t1[ci].b, bA.b], writes=[M1.b])
                                S.op(POOL, lambda e: e.tensor_tensor(out=mT[:, c, :ntok], in0=M0[:, :ntok], in1=M1[:, :ntok],
                                                                     op=ALU.add), reads=[M0.b, M1.b], adds=[mT.b])
                    for k_ in sorted(nsched):
                        for fn_ in nsched[k_]:
                            fn_()
                    with ExitStack() as pe_:
                        cnt = 0
                        for cb in range(4):
                            w = wload(wb_o, "o", 0, KC, cb * 512)
                            for si, (off, nt) in enumerate(sub):
                                bank = psum[cnt % 8]
                                XI, XO = xin[cnt % 2], xout[cnt % 2]
                                cnt += 1
                                S.dma(POOL, XI[:nt, :], xe[a + off:a + off + nt, cb * 512:(cb + 1) * 512], writes=[XI.b])
                                for kc in range(KC):
                                    S.op(PE, lambda e: e.matmul(bank.t[:nt, :], lhsT=mT[:, kc, off:off + nt], rhs=w[:, kc, :],
                                                                start=(kc == 0), stop=(kc == KC - 1)),
                                         reads=[mT.b, w.b], writes=[bank.b], inc=(kc == KC - 1))
                                S.op(DVE, lambda e: e.tensor_tensor(out=XO[:nt, :], in0=bank.t[:nt, :], in1=XI[:nt, :],
                                                                    op=ALU.add), reads=[bank.b, XI.b], writes=[XO.b])
                                m_ = a - 1024 + off
                                S.dma(ACT, X1[m_:m_ + nt, cb * 512:(cb + 1) * 512], XO[:nt, :], reads=[XO.b], adds=[x1b[j]])
                    cbarrier()
                    pde.close()
                S.barrier()

        es_r2.close()
        es_g1.close()
        del wring[:]
        for i in range(NW):
            wring.append(sb(es, f"wr3_{i}", [128, 16, 512], BF16))
        if 3 in cfg.sweeps:
            LQ["q"] = POOL
            with ExitStack() as s3:
                gb2 = sb(s3, "gb2", [128, D], F32)
                S.dma(POOL, gb2[:, :], g2[0:1, :].partition_broadcast(128), writes=[gb2.b])
                xt = [sb(s3, "xt3_%d" % i, [128, D], F32) for i in range(2)]
                hb = [sb(s3, "hb3_%d" % i, [128, D], BF16) for i in range(2)]
                ssb = [sb(s3, "ssb3_%d" % i, [128, 2], F32) for i in range(2)]
                h2T2 = [sb(s3, f"h2T{i}", [128, KC, 514], BF16) for i in range(2)]
                h2Tb2 = [[Buf(f"h2T{k}_{i}") for i in range(5)] for k in range(2)]
                tv = sb(s3, "tv", [128, 514], F32)
                aT = sb(s3, "aT", [128, FC, 512], BF16)
                NS3 = 3
                Gs = [sb(s3, "Gs%d" % i, [128, 514], F32) for i in range(NS3)]
                cA = [sb(s3, "cA%d" % i, [128, 512], F32) for i in range(NS3)]
                Vs = [sb(s3, "Vs%d" % i, [128, 512], F32) for i in range(NS3)]
                bX = [sb(s3, "bX%d" % i, [128, 512], F32) for i in range(NS3)]
                bY = [sb(s3, "bY%d" % i, [128, 512], F32) for i in range(NS3)]
                xin = [sb(s3, "xin3_%d" % i, [128, 512], F32) for i in range(2)]
                xout = [sb(s3, "xout3_%d" % i, [128, 512], F32) for i in range(2)]
                GC = 0.7978845608028654
                def s3_norm(j_):
                    mm0 = cfg.ffn_tiles[j_]
                    return norm_stages((xt, hb, ssb), lambda off, nt, mm0=mm0: X1[mm0 + off:mm0 + off + nt, :], 514,
                                       gb2, h2T2[j_ % 2], h2Tb2[j_ % 2], 0, src_reads=x1_deps(mm0, 514))

                for n1, n2 in s3_norm(0):
                    n1()
                    n2()
                for j, m0_ in enumerate(cfg.ffn_tiles):
                    h2T, h2Tb = h2T2[j % 2], h2Tb2[j % 2]
                    nsched = {}
                    if j + 1 < len(cfg.ffn_tiles):
                        for k, (n1, n2) in enumerate(s3_norm(j + 1)):
                            nsched.setdefault(1 + 2 * k, []).append(n1)
                            nsched.setdefault(2 + 2 * k, []).append(n2)
                    S.dma(POOL, tv[:, :], tokv[0:1, 1024 + m0_:1024 + m0_ + 514].partition_broadcast(128), writes=[tv.b])
                    wts = {}

                    def st0(f):
                        fb, fi = divmod(f, 4)
                        if fi == 0:
                            wts[fb] = (wload(wb_up, "up", 0, KC, fb * 512), wload(wb_up, "up", 0, KC, DFF + fb * 512))
                        wg, wv = wts[fb]
                        s_ = f % 2
                        bA, bB, bV = psum[3 * s_], psum[3 * s_ + 1], psum[3 * s_ + 2]
                        for kc in range(KC):
                            S.op(PE, lambda e: e.matmul(bA.t[:, 0:512], lhsT=wg[:, kc, fi * 128:(fi + 1) * 128],
                                                        rhs=h2T[:, kc, 0:512], start=(kc == 0), stop=(kc == KC - 1)),
                                 reads=h2Tb + [wg.b], writes=[bA.b], inc=(kc == KC - 1))
                        for kc in range(KC):
                            S.op(PE, lambda e: e.matmul(bB.t[:, 0:2], lhsT=wg[:, kc, fi * 128:(fi + 1) * 128],
                                                        rhs=h2T[:, kc, 512:514], start=(kc == 0), stop=(kc == KC - 1)),
                                 reads=h2Tb + [wg.b], writes=[bB.b], inc=(kc == KC - 1))
                        for kc in range(KC):
                            S.op(PE, lambda e: e.matmul(bV.t[:, 0:512], lhsT=wv[:, kc, fi * 128:(fi + 1) * 128],
                                                        rhs=h2T[:, kc, 1:513], start=(kc == 0), stop=(kc == KC - 1)),
                                 reads=h2Tb + [wv.b], writes=[bV.b], inc=(kc == KC - 1))

                    def st1(f):
                        s_ = f % 2
                        bA, bB, bV = psum[3 * s_], psum[3 * s_ + 1], psum[3 * s_ + 2]
                        G, CA, VS, X = Gs[f % NS3], cA[f % NS3], Vs[f % NS3], bX[f % NS3]
                        S.op(ACT, lambda e: e.activation(out=VS[:, :], in_=bV.t[:, 0:512], func=AF.Copy),
                             reads=[bV.b], writes=[VS.b])
                        S.op(DVE, lambda e: e.tensor_tensor(out=G[:, 0:512], in0=bA.t[:, 0:512], in1=tv[:, 0:512],
                                                            op=ALU.mult), reads=[bA.b, tv.b], writes=[G.b])
                        S.op(DVE, lambda e: e.tensor_tensor(out=G[:, 512:514], in0=bB.t[:, 0:2], in1=tv[:, 512:514],
                                                            op=ALU.mult), reads=[bB.b, tv.b], adds=[G.b])
                        S.op(DVE, lambda e: e.tensor_scalar(out=CA[:, :], in0=G[:, 1:513], scalar1=cw[:, 1, f:f + 1],
                                                            scalar2=cbv[:, f:f + 1], op0=ALU.mult, op1=ALU.add),
                             reads=[G.b, cw.b, cbv.b], writes=[CA.b])
                        S.op(DVE, lambda e: e.scalar_tensor_tensor(out=X[:, :], in0=G[:, 0:512], scalar=cw[:, 0, f:f + 1],
                                                                   in1=CA[:, :], op0=ALU.mult, op1=ALU.add),
                             reads=[G.b, cw.b, CA.b], writes=[X.b])
                        S.op(DVE, lambda e: e.scalar_tensor_tensor(out=CA[:, :], in0=G[:, 2:514], scalar=cw[:, 2, f:f + 1],
                                                                   in1=X[:, :], op0=ALU.mult, op1=ALU.add),
                             reads=[G.b, cw.b, X.b], writes=[CA.b])
                        S.op(POOL, lambda e: e.tensor_tensor(out=X[:, :], in0=CA[:, :], in1=CA[:, :], op=ALU.mult),
                             reads=[CA.b], writes=[X.b])

                    def st2(f):
                        G, CA, VS, X, Y = Gs[f % NS3], cA[f % NS3], Vs[f % NS3], bX[f % NS3], bY[f % NS3]
                        S.op(DVE, lambda e: e.tensor_scalar(out=Y[:, :], in0=X[:, :], scalar1=0.044715, scalar2=1.0,
                                                            op0=ALU.mult, op1=ALU.add), reads=[X.b], writes=[Y.b])
                        S.op(POOL, lambda e: e.tensor_tensor(out=X[:, :], in0=Y[:, :], in1=CA[:, :], op=ALU.mult),
                             reads=[Y.b, CA.b], writes=[X.b])
                        S.op(ACT, lambda e: e.activation(out=Y[:, :], in_=X[:, :], func=AF.Tanh, scale=GC),
                             reads=[X.b], writes=[Y.b])
                        S.op(DVE, lambda e: e.scalar_tensor_tensor(out=G[:, 0:512], in0=Y[:, :], scalar=1.0, in1=CA[:, :],
                                                                   op0=ALU.add, op1=ALU.mult),
                             reads=[Y.b, CA.b], writes=[G.b])
                        S.op(DVE, lambda e: e.scalar_tensor_tensor(out=aT[:, f, :], in0=G[:, 0:512], scalar=0.5,
                                                                   in1=VS[:, :], op0=ALU.mult, op1=ALU.mult),
                             reads=[G.b, VS.b], adds=[aT.b])

                    for t in range(FC + 2):
                        if t < FC:
                            st0(t)
                        if 0 <= t - 1 < FC:
                            st1(t - 1)
                        if 0 <= t - 2 < FC:
                            st2(t - 2)
                    sub = subtiles(512)
                    kblocks = [(0, 16), (16, 16), (32, 12)]
                    wstep = 0
                    for cb in range(4):
                        banks = [psum[4 + si] for si in range(4)]
                        for kb, (k0, nk) in enumerate(kblocks):
                            for fn_ in nsched.pop(wstep, []):
                                fn_()
                            wstep += 1
                            wd = wload(wb_down, "down", k0, nk, cb * 512)
                            for si, (off, nt) in enumerate(sub):
                                for kk in range(nk):
                                    kf_ = k0 + kk
                                    first, last = (kf_ == 0), (kf_ == FC - 1)
                                    S.op(PE, lambda e: e.matmul(banks[si].t[:nt, :], lhsT=aT[:, kf_, off:off + nt],
                                                                rhs=wd[:, kk, :], start=first, stop=last),
                                         reads=[aT.b, wd.b], writes=[banks[si].b] if first else [],
                                         adds=[] if first else [banks[si].b], inc=(kk == nk - 1))
                        if cb == 3:
                            for k_ in sorted(nsched):
                                for fn_ in nsched[k_]:
                                    fn_()
                            nsched = {}
                        for si, (off, nt) in enumerate(sub):
                            XI, XO = xin[si % 2], xout[si % 2]
                            r0 = m0_ + 1 + off
                            S.dma(POOL, XI[:nt, :], X1[r0:r0 + nt, cb * 512:(cb + 1) * 512],
                                  reads=x1_deps(m0_, 514), writes=[XI.b])
                            S.op(DVE, lambda e: e.tensor_tensor(out=XO[:nt, :], in0=banks[si].t[:nt, :], in1=XI[:nt, :],
                                                                op=ALU.add), reads=[banks[si].b, XI.b], writes=[XO.b])
                            S.dma(ACT, y[m0_ + off:m0_ + off + nt, cb * 512:(cb + 1) * 512], XO[:nt, :], reads=[XO.b])
                S.barrier()

        if 3 not in cfg.sweeps:
            zt = sb(es, "zt", [128, D], F32)
            S.op(DVE, lambda e: e.memset(zt[:, :], 0.0), writes=[zt.b])
            for r0 in range(0, NTC, 128):
                S.dma(ACT, y[r0:r0 + 128, :], zt[:, :], reads=[zt.b])

        S.barrier()
    return nc


def host_tables(cfg, seq_start, seq_len):
    NE = cfg.NE
    pos = seq_start - HALO + np.arange(NE)
    valid = ((pos >= 0) & (pos < seq_len))
    half = HD // 2
    inv_freq = (ROPE_THETA ** (-np.arange(half, dtype=np.float32) / half)).astype(np.float32)
    ang = pos.astype(np.float32)[:, None] * inv_freq[None, :]
    cos = np.cos(ang).astype(np.float32)
    sin = np.sin(ang).astype(np.float32)
    ropeC = np.concatenate([cos, cos], axis=1)
    ropeS = np.concatenate([-sin, sin], axis=1)
    invcnt = np.ones((4, NE), np.float32)
    for g, R in enumerate(POOL_R):
        lo = np.clip(pos - R, 0, seq_len)
        hi = np.clip(pos + R + 1, 0, seq_len)
        cnt = np.maximum(hi - lo, 1)
        invcnt[g] = (1.0 / cnt).astype(np.float32)
    nkt = max(len(attn_units(n)) for _, n in cfg.mix_tiles) * 2
    kvtab = np.zeros((len(cfg.mix_tiles), 128, nkt), np.float32)
    for j, (a, ntok) in enumerate(cfg.mix_tiles):
        for ui, (g, r, l0, nq) in enumerate(attn_units(ntok)):
            Dg = DIL[g]
            base = a - 64 * Dg + r + Dg * l0
            for t, (b0, nk) in enumerate(((base, 128), (base + 128 * Dg, nq))):
                u = b0 + Dg * np.arange(nk)
                ok = (u >= 0) & (u < NE)
                vv = np.zeros(nk, np.float32)
                vv[ok] = valid[u[ok]]
                kvtab[j, :nk, 2 * ui + t] = (vv - 1.0) * 30000.0
    return dict(ropeC=ropeC, ropeS=ropeS, tokvalid=valid.astype(np.float32)[None, :], invcnt=invcnt, kvtab=kvtab)


def host_consts():
    p = np.arange(128)
    m1 = (p[None, :] <= p[:, None]).astype(np.float32)
    m2 = (p[:, None] <= p[None, :]).astype(np.float32)
    negm = np.stack([np.repeat(((m1 - 1.0) * 30000.0)[:, None, :], 4, axis=1),
                     np.repeat(((m2 - 1.0) * 30000.0)[:, None, :], 4, axis=1)], axis=1).reshape(128, 1024)
    return dict(ident=np.eye(128).astype(ml_dtypes.bfloat16), ones=np.ones((128, 128), ml_dtypes.bfloat16),
                mask12=np.concatenate([m1, m2], axis=1), negm=negm.astype(ml_dtypes.bfloat16))


def make_in_maps(cfg, inputs, seqs):
    f = lambda a: np.ascontiguousarray(np.asarray(a, dtype=np.float32))
    shared = dict(
        w_in=f(inputs["w_in"][0]), w_pool=f(inputs["w_pool"][0]).reshape(POOLW, 256), w_po=f(inputs["w_pool_out"][0]),
        w_ao=f(inputs["w_attn_out"][0]), w_o=f(inputs["w_o"][0]), w_up=f(inputs["w_up"][0]),
        w_down=f(inputs["w_down"][0]), g1=f(inputs["mix_norm_g"]), g2=f(inputs["ffn_norm_g"]),
        gq=f(inputs["q_norm_g"]), gk=f(inputs["k_norm_g"]), b_gate=f(inputs["b_gate"][0]),
        pscale=f(inputs["pool_scale"][0]), conv_w=f(inputs["conv_w"][0]), conv_b=f(inputs["conv_b"][0]),
        **host_consts())
    maps = []
    for xs, start in seqs:
        L = xs.shape[0]
        xe = np.zeros((cfg.NE, D), np.float32)
        lo = start - HALO
        a, b = max(lo, 0), min(lo + cfg.NE, L)
        xe[a - lo:b - lo] = xs[a:b]
        m = dict(shared)
        m["xe"] = xe
        m.update(host_tables(cfg, start, L))
        maps.append(m)
    return maps


_CACHE = {}


def kernel(**inputs):
    cfg = Cfg(4096)
    xp = np.asarray(inputs["x_prompt"], dtype=np.float32)
    xs = np.asarray(inputs["x_sample"], dtype=np.float32)
    seqs = [(xp[b], 0) for b in range(4)] + [(xs[0], 4096 * c) for c in range(4)]
    if "nc" not in _CACHE:
        _CACHE["nc"] = build(cfg)
    in_maps = make_in_maps(cfg, inputs, seqs)
    res = run_bass_kernel_spmd(_CACHE["nc"], in_maps, core_ids=list(range(NCORES)))
    ys = [np.asarray(r["y"], dtype=np.float32) for r in res.results]
    y_prompt = np.stack(ys[:4], axis=0)
    y_sample = np.concatenate(ys[4:], axis=0)[None]
    return (y_prompt, y_sample)
```
